# Optimizing a Trainium2 kernel written in Bass

```python
import jax, jax.numpy as jnp
from jax import lax
import numpy as np

D_MODEL = 1024
BATCH = 8
SEQ = 4096
DEPTH = 4

N_HEADS = 16
HEAD_DIM = D_MODEL // N_HEADS
N_KV_HEADS = 4
GROUP = N_HEADS // N_KV_HEADS
ROT_DIM = HEAD_DIM // 4
ROPE_THETA = 500000.0
WINDOW = 128
BLOCK = 128
CONV_WIDTH = D_MODEL
CONV_K = 3
D_FF = -(-8 * D_MODEL // (3 * 256)) * 256
Q_W = N_HEADS * HEAD_DIM
KV_W = N_KV_HEADS * HEAD_DIM
IN_COLS = Q_W + 2 * KV_W + 3 * CONV_WIDTH + 2 * D_MODEL
RMS_EPS = 1e-6
NEG_INF = -1e30

kernel_name = "hybrid_gated_swa_shortconv_encoder"


def rms_norm(x, g):
    xf = x.astype(jnp.float32)
    y = xf * lax.rsqrt(jnp.mean(xf * xf, axis=-1, keepdims=True) + RMS_EPS)
    return (y * g.astype(jnp.float32)).astype(x.dtype)


def rope_tables(seq):
    pos = jnp.arange(seq, dtype=jnp.float32)
    inv = jnp.power(ROPE_THETA, -jnp.arange(0, ROT_DIM, 2, dtype=jnp.float32) / ROT_DIM)
    ang = pos[:, None] * inv[None, :]
    return jnp.cos(ang), jnp.sin(ang)


def partial_rope(t, cos, sin):
    tf = t.astype(jnp.float32)
    half = ROT_DIM // 2
    x1, x2, rest = tf[..., :half], tf[..., half:ROT_DIM], tf[..., ROT_DIM:]
    c, s = cos[None, :, None, :], sin[None, :, None, :]
    out = jnp.concatenate([x1 * c - x2 * s, x2 * c + x1 * s, rest], axis=-1)
    return out.astype(t.dtype)


def band_attention(q, k, v, sink):
    b, s = q.shape[0], q.shape[1]
    nb = s // BLOCK
    qb = q.reshape(b, nb, BLOCK, N_KV_HEADS, GROUP, HEAD_DIM).transpose(1, 0, 2, 3, 4, 5)

    def windows(t):
        tp = jnp.pad(t, ((0, 0), (BLOCK, BLOCK), (0, 0), (0, 0)))
        tb = tp.reshape(b, nb + 2, BLOCK, N_KV_HEADS, HEAD_DIM)
        w = jnp.concatenate([tb[:, :-2], tb[:, 1:-1], tb[:, 2:]], axis=2)
        return w.transpose(1, 0, 2, 3, 4)

    kw, vw = windows(k), windows(v)
    blk = jnp.arange(nb)
    q_idx = blk[:, None] * BLOCK + jnp.arange(BLOCK)[None, :]
    k_idx = (blk[:, None] - 1) * BLOCK + jnp.arange(3 * BLOCK)[None, :]
    mask = ((jnp.abs(q_idx[:, :, None] - k_idx[:, None, :]) <= WINDOW)
            & (k_idx[:, None, :] >= 0) & (k_idx[:, None, :] < s))
    sink_f = sink.astype(jnp.float32).reshape(N_KV_HEADS, GROUP)[None, :, :, None, None]
    scale = HEAD_DIM ** -0.5

    def one_block(args):
        qi, ki, vi, mi = args
        sc = jnp.einsum('bqkgd,bskd->bkgqs', qi.astype(jnp.float32), ki.astype(jnp.float32)) * scale
        sc = jnp.where(mi[None, None, None], sc, NEG_INF)
        m = jnp.maximum(jnp.max(sc, axis=-1, keepdims=True), sink_f)
        p = jnp.exp(sc - m)
        denom = jnp.sum(p, axis=-1, keepdims=True) + jnp.exp(sink_f - m)
        o = jnp.einsum('bkgqs,bskd->bqkgd', p / denom, vi.astype(jnp.float32))
        return o.astype(qi.dtype)

    o = lax.map(one_block, (qb, kw, vw, mask))
    return o.transpose(1, 0, 2, 3, 4, 5).reshape(b, s, Q_W)


def short_conv(u, w):
    up = jnp.pad(u, ((0, 0), (1, 1), (0, 0)))
    return up[:, :-2] * w[0] + up[:, 1:-1] * w[1] + up[:, 2:] * w[2]


def setup_inputs(seed: int = 0) -> dict:
    key = jax.random.key(seed)
    ks = jax.random.split(key, 16)
    f32 = jnp.float32

    def nrm(k, shape, fan_in):
        return jax.random.normal(k, shape, f32) * (fan_in ** -0.5)

    def gain(k):
        return 1.0 + 0.05 * jax.random.normal(k, (DEPTH, D_MODEL), f32)

    return {
        "x": jax.random.normal(ks[0], (BATCH, SEQ, D_MODEL), f32),
        "g_pre_mix": gain(ks[1]),
        "w_in": nrm(ks[2], (DEPTH, D_MODEL, IN_COLS), D_MODEL),
        "attn_sink": 0.5 * jax.random.normal(ks[3], (DEPTH, N_HEADS), f32),
        "conv_w": nrm(ks[4], (DEPTH, CONV_K, CONV_WIDTH), CONV_K),
        "w_attn_proj": nrm(ks[5], (DEPTH, Q_W, D_MODEL), Q_W),
        "w_conv_proj": nrm(ks[6], (DEPTH, CONV_WIDTH, D_MODEL), CONV_WIDTH),
        "w_out": nrm(ks[7], (DEPTH, D_MODEL, D_MODEL), D_MODEL),
        "g_post_mix": gain(ks[8]),
        "g_pre_ffn": gain(ks[9]),
        "w_gate": nrm(ks[10], (DEPTH, D_MODEL, D_FF), D_MODEL),
        "w_up": nrm(ks[11], (DEPTH, D_MODEL, D_FF), D_MODEL),
        "w_down": nrm(ks[12], (DEPTH, D_FF, D_MODEL), D_FF),
        "g_post_ffn": gain(ks[13]),
    }


def reference(x, g_pre_mix, w_in, attn_sink, conv_w, w_attn_proj, w_conv_proj, w_out,
              g_post_mix, g_pre_ffn, w_gate, w_up, w_down, g_post_ffn):
    b, s, _ = x.shape
    cos, sin = rope_tables(s)
    splits = np.cumsum([Q_W, KV_W, KV_W, CONV_WIDTH, CONV_WIDTH, CONV_WIDTH, D_MODEL]).tolist()
    for l in range(DEPTH):
        h = rms_norm(x, g_pre_mix[l])
        z = h @ w_in[l]
        q, k, v, cb, cc, cx, ga, gb = jnp.split(z, splits, axis=-1)
        q = partial_rope(q.reshape(b, s, N_HEADS, HEAD_DIM), cos, sin)
        k = partial_rope(k.reshape(b, s, N_KV_HEADS, HEAD_DIM), cos, sin)
        v = v.reshape(b, s, N_KV_HEADS, HEAD_DIM)
        y_attn = band_attention(q, k, v, attn_sink[l]) @ w_attn_proj[l]
        y_conv = (cb * short_conv(cc * cx, conv_w[l])) @ w_conv_proj[l]
        mix = (jax.nn.sigmoid(ga) * y_attn + jax.nn.sigmoid(gb) * y_conv) @ w_out[l]
        x = x + rms_norm(mix, g_post_mix[l])
        h = rms_norm(x, g_pre_ffn[l])
        f = (jax.nn.silu(h @ w_gate[l]) * (h @ w_up[l])) @ w_down[l]
        x = x + rms_norm(f, g_post_ffn[l])
    return x
```

```python
import contextlib
import numpy as np
import concourse.bass as bass
import concourse.mybir as mybir
from concourse.bass_utils import run_bass_kernel_spmd

F32 = mybir.dt.float32
BF16 = mybir.dt.bfloat16
ALU = mybir.AluOpType
AF = mybir.ActivationFunctionType

D = 1024
S = 4096
DEPTH = 4
NH = 16
HD = 64
NKV = 4
DFF = 2816
NFC = DFF // 128
IN_COLS = 6656
EPS = 1e-6
NT = 8
TB = 4
NBLK = 32
NW = 5

G_KV = 0
G_Q = 1
G_CB = 3
G_CC = 5
G_CX = 7
G_GA = 9
G_GB = 11
G_AP = 13
G_CP = 15
G_OUT = 17
G_G = 19
G_U = 25
G_D = 31
NG = 37

ENGS = ("pe", "act", "dve", "pool", "sp")
import os
STOP = os.environ.get("KSTOP", "")
EMBED_WAIT = os.environ.get("KEMBED", "1") == "1"


class _Stop(Exception):
    pass


CUR_LABEL = ["init"]


def ck(name):
    CUR_LABEL[0] = name
    if name == STOP:
        raise _Stop()


class Op:
    __slots__ = ("eng", "fn", "reads", "writes", "dma", "deps", "signal", "tok", "waits", "label")

    def __init__(self, eng, fn, reads, writes, dma):
        self.eng = eng
        self.fn = fn
        self.reads = reads
        self.writes = writes
        self.dma = dma
        self.deps = ()
        self.signal = False
        self.tok = None
        self.waits = ()


class Prog:
    def __init__(self, nc):
        self.nc = nc
        self.ops = []
        self.last_w = {}
        self.readers = {}

    def op(self, eng, fn, reads=(), writes=(), dma=None):
        o = Op(eng, fn, tuple(reads), tuple(writes), dma)
        o.label = CUR_LABEL[0]
        deps = set()
        for r in o.reads:
            w = self.last_w.get(r)
            if w is not None:
                deps.add(w)
        for r in o.writes:
            w = self.last_w.get(r)
            if w is not None:
                deps.add(w)
            for rd in self.readers.get(r, ()):
                deps.add(rd)
        for r in o.reads:
            self.readers.setdefault(r, []).append(o)
        for r in o.writes:
            self.last_w[r] = o
            self.readers[r] = []
        deps.discard(o)
        o.deps = [d for d in deps
                  if not (d.eng == "pe" and o.eng == "pe" and d.dma is None and o.dma is None)]
        for d in o.deps:
            d.signal = True
        self.ops.append(o)
        return o

    def emit(self, final_ops):
        nc = self.nc
        stack = contextlib.ExitStack()
        sems = {}
        cnt = {}
        for o in final_ops:
            o.signal = True
        for o in self.ops:
            if o.dma is not None:
                o.signal = True
            if not o.signal:
                continue
            key = ("dma", o.dma) if o.dma is not None else ("eng", o.eng)
            if key not in sems:
                sems[key] = stack.enter_context(nc.semaphore("s%d" % len(sems)))
                cnt[key] = 0
            cnt[key] += 16 if o.dma is not None else 1
            o.tok = (key, cnt[key])
        waited = {e: {} for e in ENGS}
        per = {e: [] for e in ENGS}
        for o in self.ops:
            w = waited[o.eng]
            need = {}
            for d in o.deps:
                k, v = d.tok
                if w.get(k, 0) >= v:
                    continue
                if need.get(k, 0) < v:
                    need[k] = v
            for k, v in need.items():
                w[k] = v
            o.waits = list(need.items())
            per[o.eng].append(o)
        fin = []
        for o in final_ops:
            fin.append(o.tok)

        def run(name):
            def f(e):
                for o in per[name]:
                    ws_ = o.waits
                    emb = None
                    if EMBED_WAIT and ws_ and o.dma is None:
                        emb = ws_[-1]
                        ws_ = ws_[:-1]
                    for k, v in ws_:
                        e.wait_ge(sems[k], v)
                    ins = o.fn(e)
                    if emb is not None:
                        ins._wait_ge(sems[emb[0]], emb[1])
                    if o.signal:
                        ins.then_inc(sems[o.tok[0]], 16 if o.dma is not None else 1)
                if name == "sp":
                    for k, v in fin:
                        e.wait_ge(sems[k], v)
            return f

        self.sem_keys = {sems[k].num: k for k in sems}
        with nc.Block() as block:
            block.tensor(run("pe"))
            block.scalar(run("act"))
            block.vector(run("dve"))
            block.gpsimd(run("pool"))
            block.sync(run("sp"))
        stack.close()


def build_program(n_layers=DEPTH):
    nc = bass.Bass("TRN2", target_bir_lowering=False)
    L = n_layers
    x_in = nc.dram_tensor("x", [S, D], F32, kind="ExternalInput").ap()
    wpack = nc.dram_tensor("wpack", [L, NG, 128, 4096], F32, kind="ExternalInput").ap()
    gcol_d = nc.dram_tensor("gcol", [L, 128, 16], F32, kind="ExternalInput").ap()
    gpost_d = nc.dram_tensor("gpost", [L, 128, 2048], F32, kind="ExternalInput").ap()
    cw_d = nc.dram_tensor("cw", [L, 128, 24], F32, kind="ExternalInput").ap()
    sink_d = nc.dram_tensor("sink", [L, 128, 16], F32, kind="ExternalInput").ap()
    cos_d = nc.dram_tensor("cos", [128, NBLK * 8], F32, kind="ExternalInput").ap()
    sin_d = nc.dram_tensor("sin", [128, NBLK * 8], F32, kind="ExternalInput").ap()
    cst_d = nc.dram_tensor("cst", [128, 384], F32, kind="ExternalInput").ap()
    out_d = nc.dram_tensor("out", [S, D], F32, kind="ExternalOutput").ap()
    wb = nc.dram_tensor("wb", [L, NG, 128, 4096], BF16, kind="Internal").ap()
    xs = nc.dram_tensor("xs", [S, D], F32, kind="Internal").ap()

    P = Prog(nc)
    es = contextlib.ExitStack()

    def sb(name, shape, dt):
        return es.enter_context(nc.sbuf_tensor(name, shape, dt))

    xt = sb("xt", [128, TB, D], F32)
    xn = sb("xn", [128, 2, D], F32)
    hbf = sb("hbf", [128, 4, D], BF16)
    stat = sb("stat", [128, 64], F32)
    hT = sb("hT", [128, 8, 640], BF16)
    kTz = sb("kTz", [128, 2, 2, 3 * 512], BF16)
    vaug = sb("vaug", [128, 12, 4, 128], BF16)
    kbf = sb("kbf", [128, 4, 256], BF16)
    rtmp = sb("rtmp", [128, 4, 64], F32)
    u1b = sb("u1b", [128, 8, 514], BF16)
    tmpf = sb("tmpf", [128, 4, 512], F32)
    u2 = sb("u2", [128, 8, 512], BF16)
    mixc = sb("mixc", [128, 8, 512], BF16)
    qbf = sb("qbf", [128, 3, 512], BF16)
    qT = sb("qT", [128, 8, 512], BF16)
    PT = sb("PT", [128, 6, 512], BF16)
    oT = sb("oT", [128, 8, 512], BF16)
    mixin = mixc
    h2T = sb("h2T", [128, 8, 512], BF16)
    aT = sb("aT", [128, NFC, 512], BF16)
    W = sb("W", [128, NW, 4096], BF16)
    gpo = sb("gpo", [128, 2048], F32)
    cos_t = sb("cos_t", [128, NBLK, 8], F32)
    sin_t = sb("sin_t", [128, NBLK, 8], F32)
    ident = sb("ident", [128, 128], BF16)
    msk = sb("msk", [128, 2, 128], BF16)
    gcol = sb("gcol_t", [128, 16], F32)
    cw = sb("cw_t", [128, 24], F32)
    snk = sb("snk", [128, 16], F32)
    epsb = sb("epsb", [128, 1], F32)
    ps = es.enter_context(nc.psum_tensor("ps", [128, 8, 512], F32))

    state = {"bank": 0, "tmp": 0, "stat": 0, "wslot": 0, "pt": 0, "hb": 0, "kbf": 0, "qbf": 0, "xn": 0, "nrm": 0, "prep": 0}

    def next_pair():
        b = state["bank"]
        if b % 2:
            b = (b + 1) % 8
        state["bank"] = (b + 2) % 8
        return b // 2

    def next_bank(kind="fm"):
        al = state.get("alloc")
        if al is not None:
            return al(kind)
        b = state["bank"]
        state["bank"] = (b + 1) % 8
        return b

    def next_tmp():
        t = state["tmp"]
        state["tmp"] = (t + 1) % 4
        return t

    def next_stat():
        s_ = state["stat"]
        state["stat"] = (s_ + 1) % 64
        return s_

    def wload(l, gid):
        s_ = state["wslot"]
        state["wslot"] = (s_ + 1) % NW
        P.op("sp", lambda e: e.dma_start(out=W[:, s_, :], in_=wb[l, gid]),
             reads=[("wb", l, gid)], writes=[("W", s_)], dma=("W", s_))
        return s_

    def Wv(s_):
        return W[:, s_, :].rearrange("p (k c) -> p k c", c=512)

    P.op("sp", lambda e: e.dma_start(out=cos_t[:].rearrange("p b i -> p (b i)"), in_=cos_d), writes=["cos"], dma="c0")
    P.op("sp", lambda e: e.dma_start(out=sin_t[:].rearrange("p b i -> p (b i)"), in_=sin_d), writes=["sin"], dma="c1")
    cstf = tmpf[:, 0, 0:384]
    P.op("sp", lambda e: e.dma_start(out=cstf, in_=cst_d), writes=[("tmp", 0)], dma="c2")
    P.op("pool", lambda e: e.memset(epsb[:], EPS), writes=["epsb"])
    P.op("dve", lambda e: e.tensor_copy(out=ident[:], in_=cstf[:, 0:128]), reads=[("tmp", 0)], writes=["ident"])
    P.op("dve", lambda e: e.tensor_copy(out=msk[:].rearrange("p a q -> p (a q)"), in_=cstf[:, 128:384]),
         reads=[("tmp", 0)], writes=["msk"])
    P.op("pool", lambda e: e.memset(kTz[:].rearrange("p a b c -> p (a b c)"), 0.0),
         writes=[("kT", i, par) for i in range(12) for par in range(2)])
    P.op("pool", lambda e: e.memset(vaug[:].rearrange("p a g c -> p (a g c)"), 1.0),
         writes=[("v", i) for i in range(12)])
    P.op("pool", lambda e: e.memset(u1b[:].rearrange("p a c -> p (a c)"), 0.0),
         writes=[("u1b", j) for j in range(8)] + ["u1h"])

    PREP_ORDER = ([G_KV, G_CC, G_CX, G_CC + 1, G_CX + 1, G_Q, G_Q + 1, G_CB, G_CB + 1, G_CP, G_GB, G_CP + 1,
                   G_GB + 1, G_AP, G_GA, G_AP + 1, G_GA + 1, G_OUT, G_OUT + 1]
                  + [g for i in range(6) for g in (G_G + i, G_U + i)] + [G_D + i for i in range(6)])
    assert sorted(PREP_ORDER) == list(range(NG))

    def prep(l, g):
        c = state["prep"]
        state["prep"] = c + 1
        P.op("pool", lambda e: e.dma_start(
            out=wb[l, g].rearrange("p (a c) -> (p a) c", c=2048),
            in_=wpack[l, g].rearrange("p (a c) -> (p a) c", c=2048)),
            writes=[("wb", l, g), ("prepslot", c % 4)], dma=("wbq", c % 4))

    def norm_p1(src_ap_fn, src_res):
        st = next_stat()
        hb = state["hb"]
        state["hb"] = (hb + 1) % 4
        P.op("act", lambda e: e.activation(out=hbf[:, hb, :], in_=src_ap_fn(), func=AF.Square, scale=1.0 / 32.0,
                                           accum_out=stat[:, st:st + 1]),
             reads=[src_res], writes=[("hbf", hb), ("stat", st)])
        P.op("act", lambda e: e.activation(out=stat[:, st:st + 1], in_=stat[:, st:st + 1], func=AF.Sqrt,
                                           bias=epsb[:, 0:1], scale=1.0),
             reads=[("stat", st), "epsb"], writes=[("stat", st)])
        P.op("dve", lambda e: e.reciprocal(out=stat[:, st:st + 1], in_=stat[:, st:st + 1]),
             reads=[("stat", st)], writes=[("stat", st)])
        P.op("act", lambda e: e.activation(out=hbf[:, hb, :], in_=src_ap_fn(), func=AF.Copy,
                                           scale=stat[:, st:st + 1]),
             reads=[src_res, ("stat", st)], writes=[("hbf", hb)])
        return hb

    def norm_p2(hb, col0, gsel, dstT, dst_res):
        for half in range(2):
            b = next_bank()
            pbf = ps[:, b, :].bitcast(BF16)
            for j in range(4):
                k = half * 4 + j
                P.op("pe", lambda e, k=k, j=j, pbf=pbf: e.transpose(
                    out=pbf[:, j * 128:(j + 1) * 128], in_=hbf[:, hb, k * 128:(k + 1) * 128], identity=ident[:]),
                    reads=[("hbf", hb), "ident"], writes=[("ps", b)])
            k0 = half * 4
            P.op("dve", lambda e, k0=k0, pbf=pbf: e.tensor_tensor(
                out=dstT[:, k0:k0 + 4, col0:col0 + 128], in0=pbf[:, 0:512].rearrange("p (j t) -> p j t", t=128),
                in1=gcol[:, gsel * 8 + k0:gsel * 8 + k0 + 4].unsqueeze(2).broadcast_to([128, 4, 128]), op=ALU.mult),
                reads=[("ps", b), "gcol"], writes=[(dst_res[0], dst_res[1], k0 + j) for j in range(4)])

    def pipeline(p1s, p2s, depth=1):
        n = len(p1s)
        outs = [None] * n
        for i in range(min(depth, n)):
            outs[i] = p1s[i]()
        for i in range(n):
            if i + depth < n:
                outs[i + depth] = p1s[i + depth]()
            p2s[i](outs[i])

    def rope(bank, nh, blk, dst_ap, dst_res, width):
        tq = next_tmp()
        P.op("act", lambda e: e.activation(out=tmpf[:, tq, 0:nh * 64], in_=ps[:, bank, 0:nh * 64], func=AF.Copy),
             reads=[("ps", bank)], writes=[("tmp", tq)])
        P.op("act", lambda e: e.activation(out=dst_ap[:, 0:width], in_=ps[:, bank, 0:width], func=AF.Copy),
             reads=[("ps", bank)], writes=[dst_res])
        src = tmpf[:, tq, 0:nh * 64].rearrange("p (h d) -> p h d", d=64)
        dv = dst_ap[:, 0:nh * 64].rearrange("p (h d) -> p h d", d=64)
        c_b = cos_t[:, blk, :].unsqueeze(1).broadcast_to([128, nh, 8])
        s_b = sin_t[:, blk, :].unsqueeze(1).broadcast_to([128, nh, 8])
        t = [rtmp[:, i, 0:nh * 8].rearrange("p (h d) -> p h d", d=8) for i in range(4)]
        x1 = src[:, :, 0:8]
        x2 = src[:, :, 8:16]
        rr = [("tmp", tq), "cos", "sin"]
        P.op("dve", lambda e: e.tensor_tensor(out=t[0], in0=x1, in1=c_b, op=ALU.mult), reads=rr, writes=["rt0"])
        P.op("dve", lambda e: e.tensor_tensor(out=t[1], in0=x2, in1=s_b, op=ALU.mult), reads=rr, writes=["rt1"])
        P.op("dve", lambda e: e.tensor_tensor(out=t[2], in0=x2, in1=c_b, op=ALU.mult), reads=rr, writes=["rt2"])
        P.op("dve", lambda e: e.tensor_tensor(out=t[3], in0=x1, in1=s_b, op=ALU.mult), reads=rr, writes=["rt3"])
        P.op("dve", lambda e: e.tensor_tensor(out=dv[:, :, 0:8], in0=t[0], in1=t[1], op=ALU.subtract),
             reads=["rt0", "rt1", dst_res], writes=[dst_res])
        P.op("dve", lambda e: e.tensor_tensor(out=dv[:, :, 8:16], in0=t[2], in1=t[3], op=ALU.add),
             reads=["rt2", "rt3", dst_res], writes=[dst_res])

    def kslot(kb):
        return ((kb // 4) % 3) * 512 + (kb % 4) * 128

    def vslot(kb):
        return ((kb // 4) % 3) * 4 + (kb % 4)

    def kv_p1(ws, c0, kb):
        b = next_bank()
        wv = Wv(ws)
        hres = [("hT", c0 // 128, k) for k in range(8)]
        for k in range(8):
            P.op("pe", lambda e, k=k: e.matmul(ps[:, b, :], lhsT=hT[:, k, c0:c0 + 128], rhs=wv[:, k, :],
                                               start=(k == 0), stop=(k == 7)),
                 reads=hres + [("W", ws)], writes=[("ps", b)])
        kk = state["kbf"]
        state["kbf"] = (kk + 1) % 4
        rope(b, 4, kb, kbf[:, kk, :], ("kbf", kk), 256)
        vs = vslot(kb)
        vsrc = ps[:, b, 256:512].rearrange("p (g two d) -> p g two d", two=2, d=64)
        P.op("act", lambda e: e.activation(out=vaug[:, vs, 0:4:2, 0:64], in_=vsrc[:, :, 0, :], func=AF.Copy),
             reads=[("ps", b)], writes=[("v", vs)])
        P.op("act", lambda e: e.activation(out=vaug[:, vs, 1:4:2, 64:128], in_=vsrc[:, :, 1, :], func=AF.Copy),
             reads=[("ps", b)], writes=[("v", vs)])
        return kk

    def kv_p2(kk, kb):
        b2 = next_bank()
        pbf = ps[:, b2, :].bitcast(BF16)
        for j in range(2):
            P.op("pe", lambda e, j=j: e.transpose(out=pbf[:, j * 128:(j + 1) * 128],
                                                  in_=kbf[:, kk, j * 128:(j + 1) * 128], identity=ident[:]),
                 reads=[("kbf", kk), "ident"], writes=[("ps", b2)])
        ks = kslot(kb)
        for par in range(2):
            rows = slice(par * 64, par * 64 + 64)
            P.op("dve", lambda e, par=par, rows=rows: e.tensor_copy(
                out=kTz[rows, par, :, ks:ks + 128], in_=pbf[rows, 0:256].rearrange("p (j t) -> p j t", t=128)),
                reads=[("ps", b2)], writes=[("kT", kb % 12, par)])

    def fm_chunk(ws, cc, rhs_fn, rhs_reads):
        b = next_bank()
        wv = Wv(ws)
        for k in range(8):
            P.op("pe", lambda e, k=k: e.matmul(ps[:, b, :], lhsT=wv[:, k, cc * 128:(cc + 1) * 128], rhs=rhs_fn(k),
                                               start=(k == 0), stop=(k == 7)),
                 reads=[("W", ws)] + rhs_reads, writes=[("ps", b)])
        return b

    hT_all = [("hT", i, k) for i in range(5) for k in range(8)]
    hT_04 = [("hT", i, k) for i in range(4) for k in range(8)]

    def make_layer(l, x_src, x_dst):
        def load_xn(blk):
            i = state["xn"]
            state["xn"] = (i + 1) % 2
            P.op("act", lambda e: e.dma_start(out=xn[:, i, :], in_=x_src[blk * 128:(blk + 1) * 128, :]),
                 reads=[("x", id(x_src), blk)], writes=[("xn", i)], dma=("xn", i))
            return i

        def build_hT(blocks):
            if not blocks:
                return
            xi = [None] * len(blocks)
            xi[0] = load_xn(blocks[0][0])

            def mk_p1(n):
                def f():
                    if n + 1 < len(blocks):
                        xi[n + 1] = load_xn(blocks[n + 1][0])
                    i = xi[n]
                    return norm_p1(lambda: xn[:, i, :], ("xn", i))
                return f

            def mk_p2(n):
                def f(hb):
                    norm_p2(hb, blocks[n][1], 0, hT, ("hT", blocks[n][2]))
                return f
            pipeline([mk_p1(n) for n in range(len(blocks))], [mk_p2(n) for n in range(len(blocks))])

        def build_hT_tile(m):
            P.op("pool", lambda e: e.tensor_copy(out=hT[:, :, 0:128], in_=hT[:, :, 512:640]),
                 reads=[("hT", 4, k) for k in range(8)], writes=[("hT", 0, k) for k in range(8)])
            blocks = [(m * TB + bi, bi * 128, bi) for bi in range(1, TB)]
            if m < NT - 1:
                blocks.append(((m + 1) * TB, 512, 4))
            build_hT(blocks)
            if m == NT - 1:
                P.op("pool", lambda e: e.memset(hT[:, :, 512:640], 0.0), writes=[("hT", 4, k) for k in range(8)])

        def tile_blocks(m):
            blocks = [(m * TB + bi, bi * 128, bi) for bi in range(1, TB)]
            if m < NT - 1:
                blocks.append(((m + 1) * TB, 512, 4))
            return blocks

        def build_hT_pre(m):
            blocks = tile_blocks(m)
            xi = {}
            hbs = {}
            for n in range(2):
                xi[n] = load_xn(blocks[n][0])
            for n in range(2):
                i = xi[n]
                hbs[n] = norm_p1(lambda i=i: xn[:, i, :], ("xn", i))
            for n in range(2, len(blocks)):
                xi[n] = load_xn(blocks[n][0])
            return blocks, xi, hbs

        def build_hT_pre2(pre):
            blocks, xi, hbs = pre
            for n in range(2, len(blocks)):
                i = xi[n]
                hbs[n] = norm_p1(lambda i=i: xn[:, i, :], ("xn", i))

        def build_hT_post(m, pre):
            blocks, xi, hbs = pre
            P.op("pool", lambda e: e.tensor_copy(out=hT[:, :, 0:128], in_=hT[:, :, 512:640]),
                 reads=[("hT", 4, k) for k in range(8)], writes=[("hT", 0, k) for k in range(8)])
            for n in range(len(blocks)):
                norm_p2(hbs[n], blocks[n][1], 0, hT, ("hT", blocks[n][2]))
            if m == NT - 1:
                P.op("pool", lambda e: e.memset(hT[:, :, 512:640], 0.0), writes=[("hT", 4, k) for k in range(8)])

        def st_A(m):
            last = (m == NT - 1)
            ck("A")
            ws = wload(l, G_KV)
            kvs = []
            for bi in range(1, TB + (0 if last else 1)):
                kvs.append((kv_p1(ws, bi * 128, m * TB + bi), m * TB + bi))

            return kvs
        def st_B(m, kvs, mid=None):
            last = (m == NT - 1)
            ck("B")
            for half in range(2):
                wcc = wload(l, G_CC + half)
                wcx = wload(l, G_CX + half)
                for c4 in range(4):
                    j = half * 4 + c4
                    if j == 5 and mid is not None:
                        mid()
                        ck("B")
                    bcc = fm_chunk(wcc, c4, lambda k: hT[:, k, 1:513], hT_all)
                    bcx = fm_chunk(wcx, c4, lambda k: hT[:, k, 1:513], hT_all)
                    if kvs and j >= 4:
                        kv_p2(*kvs.pop(0))
                    t = next_tmp()
                    P.op("act", lambda e, t=t, bcc=bcc: e.activation(out=tmpf[:, t, :], in_=ps[:, bcc, :], func=AF.Copy),
                         reads=[("ps", bcc)], writes=[("tmp", t)])
                    P.op("dve", lambda e, t=t, bcx=bcx, j=j: e.tensor_tensor(
                        out=u1b[:, j, 2:514], in0=ps[:, bcx, :], in1=tmpf[:, t, :], op=ALU.mult),
                        reads=[("ps", bcx), ("tmp", t)], writes=[("u1b", j)])
        def st_C(m):
            st = {}
            thunks = []
            for half in range(2):
                for c4 in range(4):
                    def th(half=half, c4=c4):
                        ck("C")
                        if c4 == 0:
                            st["wcb"] = wload(l, G_CB + half)
                        wcb = st["wcb"]
                        j = half * 4 + c4
                        t1 = next_tmp()
                        t2 = next_tmp()
                        P.op("dve", lambda e: e.tensor_scalar(
                            out=tmpf[:, t1, :], in0=u1b[:, j, 0:512], scalar1=cw[:, j:j + 1], scalar2=None,
                            op0=ALU.mult),
                            reads=[("u1b", j), "u1h", "cw"], writes=[("tmp", t1)])
                        P.op("dve", lambda e: e.scalar_tensor_tensor(
                            out=tmpf[:, t2, :], in0=u1b[:, j, 1:513], scalar=cw[:, 8 + j:9 + j], in1=tmpf[:, t1, :],
                            op0=ALU.mult, op1=ALU.add),
                            reads=[("u1b", j), "u1h", "cw", ("tmp", t1)], writes=[("tmp", t2)])
                        P.op("dve", lambda e: e.scalar_tensor_tensor(
                            out=tmpf[:, t1, :], in0=u1b[:, j, 2:514], scalar=cw[:, 16 + j:17 + j], in1=tmpf[:, t2, :],
                            op0=ALU.mult, op1=ALU.add),
                            reads=[("u1b", j), "cw", ("tmp", t2)], writes=[("tmp", t1)])
                        bcb = fm_chunk(wcb, c4, lambda k: hT[:, k, 0:512], hT_04)
                        P.op("dve", lambda e: e.tensor_tensor(
                            out=u2[:, j, :], in0=ps[:, bcb, :], in1=tmpf[:, t1, :], op=ALU.mult),
                            reads=[("ps", bcb), ("tmp", t1)], writes=[("u2", j)])
                        if j == 7:
                            P.op("pool", lambda e: e.tensor_copy(out=u1b[:, :, 0:2], in_=u1b[:, :, 512:514]),
                                 reads=[("u1b", jj) for jj in range(8)], writes=["u1h"])
                    thunks.append(th)
            return thunks
        def st_D(m):
            st = {}
            thunks = []
            for half in range(2):
                for c4 in range(4):
                    def t1(half=half, c4=c4):
                        ck("D")
                        if c4 == 0:
                            st["wcp"] = wload(l, G_CP + half)
                            st["wgb"] = wload(l, G_GB + half)
                        st["bg"] = fm_chunk(st["wgb"], c4, lambda k: hT[:, k, 0:512], hT_04)

                    def t2(half=half, c4=c4):
                        ck("D")
                        j = half * 4 + c4
                        bg = st["bg"]
                        by = fm_chunk(st["wcp"], c4, lambda k: u2[:, k, :], [("u2", k) for k in range(8)])
                        t = next_tmp()
                        P.op("act", lambda e: e.activation(out=tmpf[:, t, :], in_=ps[:, bg, :], func=AF.Sigmoid),
                             reads=[("ps", bg)], writes=[("tmp", t)])
                        P.op("dve", lambda e: e.tensor_tensor(
                            out=mixc[:, j, :], in0=ps[:, by, :], in1=tmpf[:, t, :], op=ALU.mult),
                            reads=[("ps", by), ("tmp", t)], writes=[("mixc", j)])
                    thunks += [t1, t2]
            return thunks
        def st_E(m, fillers=()):
            last = (m == NT - 1)
            ck("E")
            wqs = [wload(l, G_Q), wload(l, G_Q + 1)]

            def q_p1(half, bi):
                def f():
                    blk = m * TB + bi
                    wv = Wv(wqs[half])
                    b = next_bank()
                    hres = [("hT", bi, k) for k in range(8)]
                    for k in range(8):
                        P.op("pe", lambda e, k=k: e.matmul(
                            ps[:, b, :], lhsT=hT[:, k, bi * 128:(bi + 1) * 128], rhs=wv[:, k, :],
                            start=(k == 0), stop=(k == 7)),
                            reads=hres + [("W", wqs[half])], writes=[("ps", b)])
                    qq = state["qbf"]
                    state["qbf"] = (qq + 1) % 3
                    rope(b, 8, blk, qbf[:, qq, :], ("qbf", qq), 512)
                    return qq
                return f

            def q_p2(half, bi):
                def f(qq):
                    b2 = next_bank()
                    pbf = ps[:, b2, :].bitcast(BF16)
                    for j in range(4):
                        P.op("pe", lambda e, j=j: e.transpose(
                            out=pbf[:, j * 128:(j + 1) * 128], in_=qbf[:, qq, j * 128:(j + 1) * 128], identity=ident[:]),
                            reads=[("qbf", qq), "ident"], writes=[("ps", b2)])
                    P.op("act", lambda e: e.activation(
                        out=qT[:, half * 4:half * 4 + 4, bi * 128:(bi + 1) * 128],
                        in_=pbf[:, 0:512].rearrange("p (j t) -> p j t", t=128), func=AF.Copy),
                        reads=[("ps", b2)], writes=[("qT", bi, half)])
                return f
            units = [(half, bi) for half in range(2) for bi in range(TB)]
            p1s = [q_p1(*u) for u in units]
            p2s = [q_p2(*u) for u in units]
            outs = [None] * len(units)
            outs[0] = p1s[0]()
            outs[1] = p1s[1]()
            for i in range(len(units)):
                if i + 2 < len(units):
                    ck("E")
                    outs[i + 2] = p1s[i + 2]()
                if i < len(fillers):
                    fillers[i]()
                ck("E")
                p2s[i](outs[i])
        def st_F(m, fillers):
            last = (m == NT - 1)
            ck("F")
            def att_p1(bi, g):
                def f():
                    qb = m * TB + bi
                    kbs = [kb for kb in (qb - 1, qb + 1, qb) if 0 <= kb < NBLK]
                    kc = g // 2
                    pts = []
                    nmask = 0
                    for kb in kbs:
                        b = next_bank("qk")
                        ks = kslot(kb)
                        P.op("pe", lambda e, b=b, ks=ks: e.matmul(
                            ps[:, b, :], lhsT=kTz[:, g % 2, kc, ks:ks + 128],
                            rhs=qT[:, kc * 4:kc * 4 + 4, bi * 128:(bi + 1) * 128], start=True, stop=True),
                            reads=[("kT", kb % 12, g % 2), ("qT", bi, kc)], writes=[("ps", b)])
                        pi = state["pt"]
                        state["pt"] = (pi + 1) % 6
                        P.op("act", lambda e, b=b, pi=pi: e.activation(out=PT[:, pi, :], in_=ps[:, b, :], func=AF.Exp,
                                                                       scale=0.125),
                             reads=[("ps", b)], writes=[("PT", pi)])
                        if kb != qb:
                            mi = 0 if kb < qb else 1
                            ptv = PT[:, pi, :].rearrange("p (h q) -> p h q", q=128)
                            P.op("pool", lambda e, ptv=ptv, mi=mi: e.tensor_tensor(
                                out=ptv, in0=ptv, in1=msk[:, mi, :].unsqueeze(1).broadcast_to([128, 4, 128]),
                                op=ALU.mult),
                                reads=[("PT", pi), "msk"], writes=[("PT", pi)])
                            nmask += 1
                        pts.append((pi, kb))
                    return pts
                return f

            def att_p2(bi, g):
                def f(pts):
                    hr = slice((g % 2) * 64, (g % 2) * 64 + 64)
                    orr = slice(((g + 1) % 2) * 64, ((g + 1) % 2) * 64 + 64)
                    kc = g // 2
                    bo = next_bank("pv")
                    for n_, (pi, kb) in enumerate(pts):
                        vs = vslot(kb)
                        P.op("pe", lambda e, vs=vs, pi=pi, n_=n_: e.matmul(
                            ps[:, bo, :], lhsT=vaug[:, vs, g, :], rhs=PT[:, pi, :], start=(n_ == 0),
                            stop=(n_ == len(pts) - 1)),
                            reads=[("v", vs), ("PT", pi)], writes=[("ps", bo)])
                    if g % 2 == 0:
                        t = next_tmp()
                        pair["t"] = t
                    else:
                        t = pair["t"]
                    dv = tmpf[hr, t, :].rearrange("p (h q) -> p h q", q=128)
                    P.op("dve", lambda e: e.tensor_tensor(
                        out=dv, in0=ps[orr, bo, :].rearrange("p (h q) -> p h q", q=128),
                        in1=snk[orr, g * 4:g * 4 + 4].unsqueeze(2).broadcast_to([64, 4, 128]), op=ALU.add),
                        reads=[("ps", bo), "snk"], writes=[("tmp", t)])

                    def fin(bo=bo, dv=dv, hr=hr, kc=kc, g=g):
                        P.op("dve", lambda e: e.tensor_tensor(
                            out=oT[hr, kc * 4:kc * 4 + 4, bi * 128:(bi + 1) * 128],
                            in0=ps[hr, bo, :].rearrange("p (h q) -> p h q", q=128), in1=dv, op=ALU.mult),
                            reads=[("ps", bo), ("tmp", t)], writes=[("oT", bi, g)])
                    if g % 2 == 0:
                        pair["fin"] = fin
                    else:
                        P.op("dve", lambda e: e.reciprocal(out=tmpf[:, t, :], in_=tmpf[:, t, :]),
                             reads=[("tmp", t)], writes=[("tmp", t)])
                        pair["fin"]()
                        fin()
                return f
            pair = {}
            steps = [(bi, g) for bi in range(TB) for g in range(NKV)]
            p1s = [att_p1(*s_) for s_ in steps]
            p2s = [att_p2(*s_) for s_ in steps]
            cnts = {"qk": 0, "pv": 0, "fm": 0}

            def alloc(kind):
                c = cnts[kind]
                cnts[kind] = c + 1
                if kind == "qk":
                    return c % 3
                if kind == "pv":
                    return (3, 4, 7)[c % 3]
                return 5 + c % 2
            state["alloc"] = alloc
            outs = [None] * len(steps)
            outs[0] = p1s[0]()
            for i in range(len(steps)):
                if i + 1 < len(steps):
                    ck("F")
                    outs[i + 1] = p1s[i + 1]()
                if i < len(fillers):
                    fillers[i]()
                ck("F")
                p2s[i](outs[i])
            state["alloc"] = None
            state["bank"] = 0
        def st_G(m, mid=None):
            last = (m == NT - 1)
            ck("G")
            oT_all = [("oT", bi, g) for bi in range(TB) for g in range(NKV)]
            for half in range(2):
                if half == 1 and mid is not None:
                    mid()
                    ck("G")
                wap = wload(l, G_AP + half)
                wga = wload(l, G_GA + half)
                for c4 in range(4):
                    j = half * 4 + c4
                    bg = fm_chunk(wga, c4, lambda k: hT[:, k, 0:512], hT_04)
                    by = fm_chunk(wap, c4, lambda k: oT[:, k, :], oT_all)
                    t = next_tmp()
                    t2 = next_tmp()
                    P.op("act", lambda e, t=t, bg=bg: e.activation(out=tmpf[:, t, :], in_=ps[:, bg, :], func=AF.Sigmoid),
                         reads=[("ps", bg)], writes=[("tmp", t)])
                    P.op("dve", lambda e, t=t, t2=t2, by=by: e.tensor_tensor(
                        out=tmpf[:, t2, :], in0=ps[:, by, :], in1=tmpf[:, t, :], op=ALU.mult),
                        reads=[("ps", by), ("tmp", t)], writes=[("tmp", t2)])
                    P.op("pool", lambda e, t2=t2, j=j: e.tensor_tensor(
                        out=mixin[:, j, :], in0=tmpf[:, t2, :], in1=mixc[:, j, :], op=ALU.add),
                        reads=[("tmp", t2), ("mixc", j)], writes=[("mixc", j)])
        def st_H(m):
            last = (m == NT - 1)
            ck("H")
            mixin_all = [("mixc", j) for j in range(8)]
            wo = [wload(l, G_OUT), wload(l, G_OUT + 1)]

            def h_p1(bi):
                def f():
                    pr = next_pair()
                    for half in range(2):
                        b = 2 * pr + half
                        wv = Wv(wo[half])
                        for k in range(8):
                            P.op("pe", lambda e, k=k, b=b, wv=wv: e.matmul(
                                ps[:, b, :], lhsT=mixin[:, k, bi * 128:(bi + 1) * 128], rhs=wv[:, k, :],
                                start=(k == 0), stop=(k == 7)),
                                reads=mixin_all + [("W", wo[half])], writes=[("ps", b)])
                    post_norm_residual(bi, 0, pr)
                    return norm_p1(lambda: xt[:, bi, :], ("xt", bi))
                return f

            def h_p2(bi):
                def f(hb):
                    norm_p2(hb, bi * 128, 1, h2T, ("h2T", bi))
                return f
            hbs = [h_p1(bi)() for bi in range(TB)]
            return [(lambda bi=bi: h_p2(bi)(hbs[bi])) for bi in range(TB)]
        def st_FFN(m, l):
            last = (m == NT - 1)
            ck("FFN")
            if l + 1 < L:
                for g in PREP_ORDER[5 * m:5 * m + 5]:
                    prep(l + 1, g)
            h2_all = [("h2T", bi, k) for bi in range(TB) for k in range(8)]
            for gi in range(6):
                wg = wload(l, G_G + gi)
                wu = wload(l, G_U + gi)
                for c4 in range(4 if gi < 5 else 2):
                    c = gi * 4 + c4
                    bgate = fm_chunk(wg, c4, lambda k: h2T[:, k, :], h2_all)
                    bup = fm_chunk(wu, c4, lambda k: h2T[:, k, :], h2_all)
                    t = next_tmp()
                    P.op("act", lambda e, t=t, bgate=bgate: e.activation(out=tmpf[:, t, :], in_=ps[:, bgate, :],
                                                                         func=AF.Silu),
                         reads=[("ps", bgate)], writes=[("tmp", t)])
                    P.op("dve", lambda e, t=t, bup=bup, c=c: e.tensor_tensor(
                        out=aT[:, c, :], in0=ps[:, bup, :], in1=tmpf[:, t, :], op=ALU.mult),
                        reads=[("ps", bup), ("tmp", t)], writes=[("aT", c)])
            aT_all = [("aT", c) for c in range(NFC)]
            for half in range(2):
                for r in range(3):
                    wd = wload(l, G_D + half * 3 + r)
                    wv = Wv(wd)
                    nkc = 8 if r < 2 else 6
                    for bi in range(TB):
                        b = 2 * bi + half
                        for kl in range(nkc):
                            kc = r * 8 + kl
                            P.op("pe", lambda e, b=b, bi=bi, kl=kl, kc=kc, wv=wv: e.matmul(
                                ps[:, b, :], lhsT=aT[:, kc, bi * 128:(bi + 1) * 128], rhs=wv[:, kl, :],
                                start=(kc == 0), stop=(kc == NFC - 1)),
                                reads=[("aT", kc), ("W", wd)], writes=[("ps", b)])
            state["bank"] = 0
            for bi in range(TB):
                post_norm_residual(bi, 1, bi)
            P.op("pool", lambda e, m=m: e.dma_start(
                out=x_dst[m * 512:(m + 1) * 512, :].rearrange("(b p) d -> p b d", p=128), in_=xt[:]),
                reads=[("xt", bi) for bi in range(TB)],
                writes=[("x", id(x_dst), m * TB + bi) for bi in range(TB)], dma="xst")
            if not last:
                P.op("pool", lambda e, m=m: e.dma_start(
                    out=xt[:], in_=x_src[(m + 1) * 512:(m + 2) * 512, :].rearrange("(b p) d -> p b d", p=128)),
                    reads=[("x", id(x_src), (m + 1) * TB + bi) for bi in range(TB)],
                    writes=[("xt", bi) for bi in range(TB)], dma="xt")


        def load_params():
            P.op("pool", lambda e: e.dma_start(out=gcol[:], in_=gcol_d[l]), writes=["gcol"], dma="p0")
            P.op("pool", lambda e: e.dma_start(out=cw[:], in_=cw_d[l]), writes=["cw"], dma="p2")
            P.op("pool", lambda e: e.dma_start(out=snk[:], in_=sink_d[l]), writes=["snk0"], dma="p3")
            P.op("act", lambda e: e.activation(out=snk[:], in_=snk[:], func=AF.Exp), reads=["snk0"], writes=["snk0", "snk"])

        def prologue():
            ck("start")
            if l > 0:
                load_params()

            build_hT([(0, 512, 4)])
            ws = wload(l, G_KV)
            kk0 = kv_p1(ws, 512, 0)
            kv_p2(kk0, 0)
            P.op("pool", lambda e: e.memset(u1b[:, :, 0:2], 0.0), writes=["u1h"])
            for half in range(2):
                wcc = wload(l, G_CC + half)
                wcx = wload(l, G_CX + half)
                bcc = next_bank()
                bcx = next_bank()
                for (bb, wsl) in ((bcc, wcc), (bcx, wcx)):
                    wv = Wv(wsl)
                    for c4 in range(4):
                        for k in range(8):
                            P.op("pe", lambda e, k=k, c4=c4, wv=wv, bb=bb: e.matmul(
                                ps[:, bb, c4:c4 + 1], lhsT=wv[:, k, c4 * 128:(c4 + 1) * 128], rhs=hT[:, k, 512:513],
                                start=(k == 0), stop=(k == 7)),
                                reads=[("W", wsl)] + [("hT", 4, k2) for k2 in range(8)], writes=[("ps", bb)])
                t = next_tmp()
                P.op("act", lambda e, t=t, bcc=bcc: e.activation(out=tmpf[:, t, 0:4], in_=ps[:, bcc, 0:4], func=AF.Copy),
                     reads=[("ps", bcc)], writes=[("tmp", t)])
                P.op("dve", lambda e, t=t, bcx=bcx, half=half: e.tensor_tensor(
                    out=u1b[:, half * 4:half * 4 + 4, 1], in0=ps[:, bcx, 0:4], in1=tmpf[:, t, 0:4], op=ALU.mult),
                    reads=[("ps", bcx), ("tmp", t)], writes=["u1h"])
            build_hT_tile(0)

            st_B(0, st_A(0))

        def body(next_prologue):
            P.op("pool", lambda e: e.dma_start(out=gpo[:], in_=gpost_d[l]), writes=["gpo"], dma="p1")
            P.op("pool", lambda e: e.dma_start(
                out=xt[:], in_=x_src[0:512, :].rearrange("(b p) d -> p b d", p=128)),
                reads=[("x", id(x_src), bi) for bi in range(TB)],
                writes=[("xt", bi) for bi in range(TB)], dma="xt")

            for m in range(NT):
                last = (m == NT - 1)
                st_E(m, st_C(m))
                st_F(m, st_D(m))
                pre = None if last else build_hT_pre(m + 1)
                st_G(m, None if last else (lambda: build_hT_pre2(pre)))
                if not last:
                    build_hT_post(m + 1, pre)
                h2 = st_H(m)

                def run_h2():
                    ck("H2")
                    for f in h2:
                        f()
                if not last:
                    st_B(m + 1, st_A(m + 1), run_h2)
                else:
                    run_h2()
                if last and next_prologue is not None:
                    next_prologue()
                st_FFN(m, l)


        return prologue, body, load_params

    def post_norm_residual(bi, which, pr):
        st = next_stat()
        src = ps[:, 2 * pr:2 * pr + 2, :].rearrange("p a c -> p (a c)")
        rr = [("ps", 2 * pr), ("ps", 2 * pr + 1)]
        tp = state["nrm"]
        state["nrm"] = 1 - tp
        nrm = tmpf[:, 2 * tp:2 * tp + 2, :].rearrange("p a c -> p (a c)")
        nres = [("tmp", 2 * tp), ("tmp", 2 * tp + 1)]
        P.op("act", lambda e: e.activation(out=nrm, in_=src, func=AF.Square, scale=1.0 / 32.0,
                                           accum_out=stat[:, st:st + 1]),
             reads=rr, writes=nres + [("stat", st)])
        P.op("act", lambda e: e.activation(out=stat[:, st:st + 1], in_=stat[:, st:st + 1], func=AF.Sqrt,
                                           bias=epsb[:, 0:1], scale=1.0),
             reads=[("stat", st), "epsb"], writes=[("stat", st)])
        P.op("dve", lambda e: e.reciprocal(out=stat[:, st:st + 1], in_=stat[:, st:st + 1]),
             reads=[("stat", st)], writes=[("stat", st)])
        P.op("dve", lambda e: e.scalar_tensor_tensor(
            out=nrm, in0=src, scalar=stat[:, st:st + 1], in1=gpo[:, which * 1024:(which + 1) * 1024],
            op0=ALU.mult, op1=ALU.mult),
            reads=rr + [("stat", st), "gpo"], writes=nres)
        P.op("pool", lambda e: e.tensor_tensor(out=xt[:, bi, :], in0=xt[:, bi, :], in1=nrm, op=ALU.add),
             reads=nres + [("xt", bi)], writes=[("xt", bi)])

    try:
        lays = [make_layer(l, x_in if l == 0 else xs, out_d if l == L - 1 else xs) for l in range(L)]
        lays[0][2]()
        for g in PREP_ORDER[:5]:
            prep(0, g)
        lays[0][0]()
        for g in PREP_ORDER[5:]:
            prep(0, g)
        for l in range(L):
            lays[l][1](lays[l + 1][0] if l + 1 < L else None)
    except _Stop:
        pass
    final = [o for o in P.ops if o.dma == "xst"][-1:]
    if not final:
        final = [o for o in P.ops if o.dma is not None][-1:]
    P.emit(final)
    es.close()
    nc._prog_ops = P.ops
    nc._sem_keys = P.sem_keys
    return nc


def _head_perm():
    heads = []
    for j in range(8):
        if j < 4:
            heads += [j, 4 + j]
        else:
            heads += [8 + (j - 4), 12 + (j - 4)]
    idx = np.concatenate([np.arange(h * 64, (h + 1) * 64) for h in heads])
    return idx


def _grp(mat, c0, ncols=512):
    k = mat.shape[0] // 128
    g = np.zeros((128, 8, 512), np.float32)
    blk = mat[:, c0:c0 + ncols].reshape(k, 128, -1).transpose(1, 0, 2)
    g[:, :k, :blk.shape[2]] = blk
    return g.reshape(128, 4096)


def pack_weights(w_in, w_attn_proj, w_conv_proj, w_out, w_gate, w_up, w_down, n_layers):
    perm = _head_perm()
    out = np.zeros((n_layers, NG, 128, 4096), np.float32)
    for l in range(n_layers):
        wi = w_in[l]
        q = wi[:, 0:1024][:, perm]
        out[l, G_KV] = _grp(wi, 1024)
        for h in range(2):
            out[l, G_Q + h] = _grp(q, h * 512)
            out[l, G_CB + h] = _grp(wi, 1536 + h * 512)
            out[l, G_CC + h] = _grp(wi, 2560 + h * 512)
            out[l, G_CX + h] = _grp(wi, 3584 + h * 512)
            out[l, G_GA + h] = _grp(wi, 4608 + h * 512)
            out[l, G_GB + h] = _grp(wi, 5632 + h * 512)
            out[l, G_AP + h] = _grp(w_attn_proj[l][perm, :], h * 512)
            out[l, G_CP + h] = _grp(w_conv_proj[l], h * 512)
            out[l, G_OUT + h] = _grp(w_out[l], h * 512)
        for gi in range(6):
            out[l, G_G + gi] = _grp(w_gate[l], gi * 512)
            out[l, G_U + gi] = _grp(w_up[l], gi * 512)
        for h in range(2):
            for r in range(3):
                rows = w_down[l][r * 1024:min((r + 1) * 1024, DFF), :]
                out[l, G_D + h * 3 + r] = _grp(rows, h * 512)
    return out


def host_consts():
    pos = np.arange(S, dtype=np.float32)
    inv = np.power(np.float32(500000.0), -np.arange(0, 16, 2, dtype=np.float32) / np.float32(16)).astype(np.float32)
    ang = (pos[:, None] * inv[None, :]).astype(np.float32)
    cos = np.cos(ang).astype(np.float32).reshape(NBLK, 128, 8).transpose(1, 0, 2).reshape(128, NBLK * 8)
    sin = np.sin(ang).astype(np.float32).reshape(NBLK, 128, 8).transpose(1, 0, 2).reshape(128, NBLK * 8)
    kk = np.arange(128)[:, None]
    qq = np.arange(128)[None, :]
    cst = np.concatenate([np.eye(128, dtype=np.float32), (kk >= qq).astype(np.float32),
                          (kk <= qq).astype(np.float32)], axis=1)
    return np.ascontiguousarray(cos), np.ascontiguousarray(sin), np.ascontiguousarray(cst)


def small_params(g_pre_mix, g_post_mix, g_pre_ffn, g_post_ffn, conv_w, attn_sink, n_layers):
    L = n_layers
    gcol = np.zeros((L, 128, 16), np.float32)
    gpost = np.zeros((L, 128, 2048), np.float32)
    cw = np.zeros((L, 128, 24), np.float32)
    sink = np.zeros((L, 128, 16), np.float32)
    for l in range(L):
        gcol[l, :, 0:8] = g_pre_mix[l].reshape(8, 128).T
        gcol[l, :, 8:16] = g_pre_ffn[l].reshape(8, 128).T
        gpost[l, :, 0:1024] = g_post_mix[l][None, :]
        gpost[l, :, 1024:2048] = g_post_ffn[l][None, :]
        for t in range(3):
            cw[l, :, t * 8:(t + 1) * 8] = conv_w[l, t].reshape(8, 128).T
        sink[l] = attn_sink[l][None, :]
    return gcol, gpost, cw, sink


_NC_CACHE = {}


def run_layers(x, params, n_layers, n_cores=8, trace=False):
    (g_pre_mix, w_in, attn_sink, conv_w, w_attn_proj, w_conv_proj, w_out, g_post_mix, g_pre_ffn,
     w_gate, w_up, w_down, g_post_ffn) = [np.asarray(p, np.float32) for p in params]
    if n_layers not in _NC_CACHE:
        _NC_CACHE[n_layers] = build_program(n_layers)
    nc = _NC_CACHE[n_layers]
    wpack = pack_weights(w_in, w_attn_proj, w_conv_proj, w_out, w_gate, w_up, w_down, n_layers)
    cos, sin, cst = host_consts()
    gcol, gpost, cw, sink = small_params(g_pre_mix, g_post_mix, g_pre_ffn, g_post_ffn, conv_w, attn_sink, n_layers)
    in_maps = []
    for c in range(n_cores):
        in_maps.append({"x": np.ascontiguousarray(x[c]), "wpack": wpack, "gcol": gcol, "gpost": gpost, "cw": cw,
                        "sink": sink, "cos": cos, "sin": sin, "cst": cst})
    res = run_bass_kernel_spmd(nc, in_maps, core_ids=list(range(n_cores)), trace=trace)
    out = np.stack([np.asarray(r["out"]) for r in res.results], axis=0)
    return out, res


def kernel(x, g_pre_mix, w_in, attn_sink, conv_w, w_attn_proj, w_conv_proj, w_out,
           g_post_mix, g_pre_ffn, w_gate, w_up, w_down, g_post_ffn):
    x = np.asarray(x, np.float32)
    params = (g_pre_mix, w_in, attn_sink, conv_w, w_attn_proj, w_conv_proj, w_out, g_post_mix, g_pre_ffn,
              w_gate, w_up, w_down, g_post_ffn)
    out, _ = run_layers(x, params, DEPTH, n_cores=8)
    return out.astype(np.float32)
```

```python
import contextlib
import numpy as np
import concourse.bass as bass
import concourse.mybir as mybir
from concourse.bass_utils import run_bass_kernel_spmd

F32 = mybir.dt.float32
BF16 = mybir.dt.bfloat16
ALU = mybir.AluOpType
AF = mybir.ActivationFunctionType

D = 1024
S = 4096
DEPTH = 4
NH = 16
HD = 64
NKV = 4
DFF = 2816
NFC = DFF // 128
IN_COLS = 6656
EPS = 1e-6
NT = 8
TB = 4
NBLK = 32
NW = 5

G_KV = 0
G_Q = 1
G_CB = 3
G_CC = 5
G_CX = 7
G_GA = 9
G_GB = 11
G_AP = 13
G_CP = 15
G_OUT = 17
G_G = 19
G_U = 25
G_D = 31
NG = 37

ENGS = ("pe", "act", "dve", "pool", "sp")
import os
STOP = os.environ.get("KSTOP", "")
EMBED_WAIT = os.environ.get("KEMBED", "1") == "1"


class _Stop(Exception):
    pass


CUR_LABEL = ["init"]


def ck(name):
    CUR_LABEL[0] = name
    if name == STOP:
        raise _Stop()


class Op:
    __slots__ = ("eng", "fn", "reads", "writes", "dma", "deps", "signal", "tok", "waits", "label")

    def __init__(self, eng, fn, reads, writes, dma):
        self.eng = eng
        self.fn = fn
        self.reads = reads
        self.writes = writes
        self.dma = dma
        self.deps = ()
        self.signal = False
        self.tok = None
        self.waits = ()


class Prog:
    def __init__(self, nc):
        self.nc = nc
        self.ops = []
        self.last_w = {}
        self.readers = {}

    def op(self, eng, fn, reads=(), writes=(), dma=None):
        o = Op(eng, fn, tuple(reads), tuple(writes), dma)
        o.label = CUR_LABEL[0]
        deps = set()
        for r in o.reads:
            w = self.last_w.get(r)
            if w is not None:
                deps.add(w)
        for r in o.writes:
            w = self.last_w.get(r)
            if w is not None:
                deps.add(w)
            for rd in self.readers.get(r, ()):
                deps.add(rd)
        for r in o.reads:
            self.readers.setdefault(r, []).append(o)
        for r in o.writes:
            self.last_w[r] = o
            self.readers[r] = []
        deps.discard(o)
        o.deps = [d for d in deps
                  if not (d.eng == "pe" and o.eng == "pe" and d.dma is None and o.dma is None)]
        for d in o.deps:
            d.signal = True
        self.ops.append(o)
        return o

    def emit(self, final_ops):
        nc = self.nc
        stack = contextlib.ExitStack()
        sems = {}
        cnt = {}
        for o in final_ops:
            o.signal = True
        for o in self.ops:
            if o.dma is not None:
                o.signal = True
            if not o.signal:
                continue
            key = ("dma", o.dma) if o.dma is not None else ("eng", o.eng)
            if key not in sems:
                sems[key] = stack.enter_context(nc.semaphore("s%d" % len(sems)))
                cnt[key] = 0
            cnt[key] += 16 if o.dma is not None else 1
            o.tok = (key, cnt[key])
        waited = {e: {} for e in ENGS}
        per = {e: [] for e in ENGS}
        for o in self.ops:
            w = waited[o.eng]
            need = {}
            for d in o.deps:
                k, v = d.tok
                if w.get(k, 0) >= v:
                    continue
                if need.get(k, 0) < v:
                    need[k] = v
            for k, v in need.items():
                w[k] = v
            o.waits = list(need.items())
            per[o.eng].append(o)
        fin = []
        for o in final_ops:
            fin.append(o.tok)

        def run(name):
            def f(e):
                for o in per[name]:
                    ws_ = o.waits
                    emb = None
                    if EMBED_WAIT and ws_ and o.dma is None:
                        emb = ws_[-1]
                        ws_ = ws_[:-1]
                    for k, v in ws_:
                        e.wait_ge(sems[k], v)
                    ins = o.fn(e)
                    if emb is not None:
                        ins._wait_ge(sems[emb[0]], emb[1])
                    if o.signal:
                        ins.then_inc(sems[o.tok[0]], 16 if o.dma is not None else 1)
                if name == "sp":
                    for k, v in fin:
                        e.wait_ge(sems[k], v)
            return f

        self.sem_keys = {sems[k].num: k for k in sems}
        with nc.Block() as block:
            block.tensor(run("pe"))
            block.scalar(run("act"))
            block.vector(run("dve"))
            block.gpsimd(run("pool"))
            block.sync(run("sp"))
        stack.close()


def build_program(n_layers=DEPTH):
    nc = bass.Bass("TRN2", target_bir_lowering=False)
    L = n_layers
    x_in = nc.dram_tensor("x", [S, D], F32, kind="ExternalInput").ap()
    wpack = nc.dram_tensor("wpack", [L, NG, 128, 4096], F32, kind="ExternalInput").ap()
    gcol_d = nc.dram_tensor("gcol", [L, 128, 16], F32, kind="ExternalInput").ap()
    gpost_d = nc.dram_tensor("gpost", [L, 128, 2048], F32, kind="ExternalInput").ap()
    cw_d = nc.dram_tensor("cw", [L, 128, 24], F32, kind="ExternalInput").ap()
    sink_d = nc.dram_tensor("sink", [L, 128, 16], F32, kind="ExternalInput").ap()
    cos_d = nc.dram_tensor("cos", [128, NBLK * 8], F32, kind="ExternalInput").ap()
    sin_d = nc.dram_tensor("sin", [128, NBLK * 8], F32, kind="ExternalInput").ap()
    cst_d = nc.dram_tensor("cst", [128, 384], F32, kind="ExternalInput").ap()
    out_d = nc.dram_tensor("out", [S, D], F32, kind="ExternalOutput").ap()
    wb = nc.dram_tensor("wb", [L, NG, 128, 4096], BF16, kind="Internal").ap()
    xs = nc.dram_tensor("xs", [S, D], F32, kind="Internal").ap()

    P = Prog(nc)
    es = contextlib.ExitStack()

    def sb(name, shape, dt):
        return es.enter_context(nc.sbuf_tensor(name, shape, dt))

    xt = sb("xt", [128, TB, D], F32)
    xn = sb("xn", [128, 2, D], F32)
    hbf = sb("hbf", [128, 4, D], BF16)
    stat = sb("stat", [128, 64], F32)
    hT = sb("hT", [128, 8, 640], BF16)
    kTz = sb("kTz", [128, 2, 2, 3 * 512], BF16)
    vaug = sb("vaug", [128, 12, 4, 128], BF16)
    kbf = sb("kbf", [128, 4, 256], BF16)
    rtmp = sb("rtmp", [128, 4, 64], F32)
    u1b = sb("u1b", [128, 8, 514], BF16)
    tmpf = sb("tmpf", [128, 4, 512], F32)
    u2 = sb("u2", [128, 8, 512], BF16)
    mixc = sb("mixc", [128, 8, 512], BF16)
    qbf = sb("qbf", [128, 3, 512], BF16)
    qT = sb("qT", [128, 8, 512], BF16)
    PT = sb("PT", [128, 6, 512], BF16)
    oT = sb("oT", [128, 8, 512], BF16)
    mixin = mixc
    h2T = sb("h2T", [128, 8, 512], BF16)
    aT = sb("aT", [128, NFC, 512], BF16)
    W = sb("W", [128, NW, 4096], BF16)
    gpo = sb("gpo", [128, 2048], F32)
    cos_t = sb("cos_t", [128, NBLK, 8], F32)
    sin_t = sb("sin_t", [128, NBLK, 8], F32)
    ident = sb("ident", [128, 128], BF16)
    msk = sb("msk", [128, 2, 128], BF16)
    gcol = sb("gcol_t", [128, 16], F32)
    cw = sb("cw_t", [128, 24], F32)
    snk = sb("snk", [128, 16], F32)
    epsb = sb("epsb", [128, 1], F32)
    ps = es.enter_context(nc.psum_tensor("ps", [128, 8, 512], F32))

    state = {"bank": 0, "tmp": 0, "stat": 0, "wslot": 0, "pt": 0, "hb": 0, "kbf": 0, "qbf": 0, "xn": 0, "nrm": 0, "prep": 0}

    def next_pair():
        b = state["bank"]
        if b % 2:
            b = (b + 1) % 8
        state["bank"] = (b + 2) % 8
        return b // 2

    def next_bank(kind="fm"):
        al = state.get("alloc")
        if al is not None:
            return al(kind)
        b = state["bank"]
        state["bank"] = (b + 1) % 8
        return b

    def next_tmp():
        t = state["tmp"]
        state["tmp"] = (t + 1) % 4
        return t

    def next_stat():
        s_ = state["stat"]
        state["stat"] = (s_ + 1) % 64
        return s_

    def wload(l, gid):
        s_ = state["wslot"]
        state["wslot"] = (s_ + 1) % NW
        P.op("sp", lambda e: e.dma_start(out=W[:, s_, :], in_=wb[l, gid]),
             reads=[("wb", l, gid)], writes=[("W", s_)], dma=("W", s_))
        return s_

    def Wv(s_):
        return W[:, s_, :].rearrange("p (k c) -> p k c", c=512)

    P.op("sp", lambda e: e.dma_start(out=cos_t[:].rearrange("p b i -> p (b i)"), in_=cos_d), writes=["cos"], dma="c0")
    P.op("sp", lambda e: e.dma_start(out=sin_t[:].rearrange("p b i -> p (b i)"), in_=sin_d), writes=["sin"], dma="c1")
    cstf = tmpf[:, 0, 0:384]
    P.op("sp", lambda e: e.dma_start(out=cstf, in_=cst_d), writes=[("tmp", 0)], dma="c2")
    P.op("pool", lambda e: e.memset(epsb[:], EPS), writes=["epsb"])
    P.op("dve", lambda e: e.tensor_copy(out=ident[:], in_=cstf[:, 0:128]), reads=[("tmp", 0)], writes=["ident"])
    P.op("dve", lambda e: e.tensor_copy(out=msk[:].rearrange("p a q -> p (a q)"), in_=cstf[:, 128:384]),
         reads=[("tmp", 0)], writes=["msk"])
    P.op("pool", lambda e: e.memset(kTz[:].rearrange("p a b c -> p (a b c)"), 0.0),
         writes=[("kT", i, par) for i in range(12) for par in range(2)])
    P.op("pool", lambda e: e.memset(vaug[:].rearrange("p a g c -> p (a g c)"), 1.0),
         writes=[("v", i) for i in range(12)])
    P.op("pool", lambda e: e.memset(u1b[:].rearrange("p a c -> p (a c)"), 0.0),
         writes=[("u1b", j) for j in range(8)] + ["u1h"])

    PREP_ORDER = ([G_KV, G_CC, G_CX, G_CC + 1, G_CX + 1, G_Q, G_Q + 1, G_CB, G_CB + 1, G_CP, G_GB, G_CP + 1,
                   G_GB + 1, G_AP, G_GA, G_AP + 1, G_GA + 1, G_OUT, G_OUT + 1]
                  + [g for i in range(6) for g in (G_G + i, G_U + i)] + [G_D + i for i in range(6)])
    assert sorted(PREP_ORDER) == list(range(NG))

    def prep(l, g):
        c = state["prep"]
        state["prep"] = c + 1
        P.op("pool", lambda e: e.dma_start(
            out=wb[l, g].rearrange("p (a c) -> (p a) c", c=2048),
            in_=wpack[l, g].rearrange("p (a c) -> (p a) c", c=2048)),
            writes=[("wb", l, g), ("prepslot", c % 4)], dma=("wbq", c % 4))

    def norm_p1(src_ap_fn, src_res):
        st = next_stat()
        hb = state["hb"]
        state["hb"] = (hb + 1) % 4
        P.op("act", lambda e: e.activation(out=hbf[:, hb, :], in_=src_ap_fn(), func=AF.Square, scale=1.0 / 32.0,
                                           accum_out=stat[:, st:st + 1]),
             reads=[src_res], writes=[("hbf", hb), ("stat", st)])
        P.op("act", lambda e: e.activation(out=stat[:, st:st + 1], in_=stat[:, st:st + 1], func=AF.Sqrt,
                                           bias=epsb[:, 0:1], scale=1.0),
             reads=[("stat", st), "epsb"], writes=[("stat", st)])
        P.op("dve", lambda e: e.reciprocal(out=stat[:, st:st + 1], in_=stat[:, st:st + 1]),
             reads=[("stat", st)], writes=[("stat", st)])
        P.op("act", lambda e: e.activation(out=hbf[:, hb, :], in_=src_ap_fn(), func=AF.Copy,
                                           scale=stat[:, st:st + 1]),
             reads=[src_res, ("stat", st)], writes=[("hbf", hb)])
        return hb

    def norm_p2(hb, col0, gsel, dstT, dst_res):
        for half in range(2):
            b = next_bank()
            pbf = ps[:, b, :].bitcast(BF16)
            for j in range(4):
                k = half * 4 + j
                P.op("pe", lambda e, k=k, j=j, pbf=pbf: e.transpose(
                    out=pbf[:, j * 128:(j + 1) * 128], in_=hbf[:, hb, k * 128:(k + 1) * 128], identity=ident[:]),
                    reads=[("hbf", hb), "ident"], writes=[("ps", b)])
            k0 = half * 4
            P.op("dve", lambda e, k0=k0, pbf=pbf: e.tensor_tensor(
                out=dstT[:, k0:k0 + 4, col0:col0 + 128], in0=pbf[:, 0:512].rearrange("p (j t) -> p j t", t=128),
                in1=gcol[:, gsel * 8 + k0:gsel * 8 + k0 + 4].unsqueeze(2).broadcast_to([128, 4, 128]), op=ALU.mult),
                reads=[("ps", b), "gcol"], writes=[(dst_res[0], dst_res[1], k0 + j) for j in range(4)])

    def pipeline(p1s, p2s, depth=1):
        n = len(p1s)
        outs = [None] * n
        for i in range(min(depth, n)):
            outs[i] = p1s[i]()
        for i in range(n):
            if i + depth < n:
                outs[i + depth] = p1s[i + depth]()
            p2s[i](outs[i])

    def rope(bank, nh, blk, dst_ap, dst_res, width):
        tq = next_tmp()
        P.op("act", lambda e: e.activation(out=tmpf[:, tq, 0:nh * 64], in_=ps[:, bank, 0:nh * 64], func=AF.Copy),
             reads=[("ps", bank)], writes=[("tmp", tq)])
        P.op("act", lambda e: e.activation(out=dst_ap[:, 0:width], in_=ps[:, bank, 0:width], func=AF.Copy),
             reads=[("ps", bank)], writes=[dst_res])
        src = tmpf[:, tq, 0:nh * 64].rearrange("p (h d) -> p h d", d=64)
        dv = dst_ap[:, 0:nh * 64].rearrange("p (h d) -> p h d", d=64)
        c_b = cos_t[:, blk, :].unsqueeze(1).broadcast_to([128, nh, 8])
        s_b = sin_t[:, blk, :].unsqueeze(1).broadcast_to([128, nh, 8])
        t = [rtmp[:, i, 0:nh * 8].rearrange("p (h d) -> p h d", d=8) for i in range(4)]
        x1 = src[:, :, 0:8]
        x2 = src[:, :, 8:16]
        rr = [("tmp", tq), "cos", "sin"]
        P.op("dve", lambda e: e.tensor_tensor(out=t[0], in0=x1, in1=c_b, op=ALU.mult), reads=rr, writes=["rt0"])
        P.op("dve", lambda e: e.tensor_tensor(out=t[1], in0=x2, in1=s_b, op=ALU.mult), reads=rr, writes=["rt1"])
        P.op("dve", lambda e: e.tensor_tensor(out=t[2], in0=x2, in1=c_b, op=ALU.mult), reads=rr, writes=["rt2"])
        P.op("dve", lambda e: e.tensor_tensor(out=t[3], in0=x1, in1=s_b, op=ALU.mult), reads=rr, writes=["rt3"])
        P.op("dve", lambda e: e.tensor_tensor(out=dv[:, :, 0:8], in0=t[0], in1=t[1], op=ALU.subtract),
             reads=["rt0", "rt1", dst_res], writes=[dst_res])
        P.op("dve", lambda e: e.tensor_tensor(out=dv[:, :, 8:16], in0=t[2], in1=t[3], op=ALU.add),
             reads=["rt2", "rt3", dst_res], writes=[dst_res])

    def kslot(kb):
        return ((kb // 4) % 3) * 512 + (kb % 4) * 128

    def vslot(kb):
        return ((kb // 4) % 3) * 4 + (kb % 4)

    def kv_p1(ws, c0, kb):
        b = next_bank()
        wv = Wv(ws)
        hres = [("hT", c0 // 128, k) for k in range(8)]
        for k in range(8):
            P.op("pe", lambda e, k=k: e.matmul(ps[:, b, :], lhsT=hT[:, k, c0:c0 + 128], rhs=wv[:, k, :],
                                               start=(k == 0), stop=(k == 7)),
                 reads=hres + [("W", ws)], writes=[("ps", b)])
        kk = state["kbf"]
        state["kbf"] = (kk + 1) % 4
        rope(b, 4, kb, kbf[:, kk, :], ("kbf", kk), 256)
        vs = vslot(kb)
        vsrc = ps[:, b, 256:512].rearrange("p (g two d) -> p g two d", two=2, d=64)
        P.op("act", lambda e: e.activation(out=vaug[:, vs, 0:4:2, 0:64], in_=vsrc[:, :, 0, :], func=AF.Copy),
             reads=[("ps", b)], writes=[("v", vs)])
        P.op("act", lambda e: e.activation(out=vaug[:, vs, 1:4:2, 64:128], in_=vsrc[:, :, 1, :], func=AF.Copy),
             reads=[("ps", b)], writes=[("v", vs)])
        return kk

    def kv_p2(kk, kb):
        b2 = next_bank()
        pbf = ps[:, b2, :].bitcast(BF16)
        for j in range(2):
            P.op("pe", lambda e, j=j: e.transpose(out=pbf[:, j * 128:(j + 1) * 128],
                                                  in_=kbf[:, kk, j * 128:(j + 1) * 128], identity=ident[:]),
                 reads=[("kbf", kk), "ident"], writes=[("ps", b2)])
        ks = kslot(kb)
        for par in range(2):
            rows = slice(par * 64, par * 64 + 64)
            P.op("dve", lambda e, par=par, rows=rows: e.tensor_copy(
                out=kTz[rows, par, :, ks:ks + 128], in_=pbf[rows, 0:256].rearrange("p (j t) -> p j t", t=128)),
                reads=[("ps", b2)], writes=[("kT", kb % 12, par)])

    def fm_chunk(ws, cc, rhs_fn, rhs_reads):
        b = next_bank()
        wv = Wv(ws)
        for k in range(8):
            P.op("pe", lambda e, k=k: e.matmul(ps[:, b, :], lhsT=wv[:, k, cc * 128:(cc + 1) * 128], rhs=rhs_fn(k),
                                               start=(k == 0), stop=(k == 7)),
                 reads=[("W", ws)] + rhs_reads, writes=[("ps", b)])
        return b

    hT_all = [("hT", i, k) for i in range(5) for k in range(8)]
    hT_04 = [("hT", i, k) for i in range(4) for k in range(8)]

    def make_layer(l, x_src, x_dst):
        def load_xn(blk):
            i = state["xn"]
            state["xn"] = (i + 1) % 2
            P.op("act", lambda e: e.dma_start(out=xn[:, i, :], in_=x_src[blk * 128:(blk + 1) * 128, :]),
                 reads=[("x", id(x_src), blk)], writes=[("xn", i)], dma=("xn", i))
            return i

        def build_hT(blocks):
            if not blocks:
                return
            xi = [None] * len(blocks)
            xi[0] = load_xn(blocks[0][0])

            def mk_p1(n):
                def f():
                    if n + 1 < len(blocks):
                        xi[n + 1] = load_xn(blocks[n + 1][0])
                    i = xi[n]
                    return norm_p1(lambda: xn[:, i, :], ("xn", i))
                return f

            def mk_p2(n):
                def f(hb):
                    norm_p2(hb, blocks[n][1], 0, hT, ("hT", blocks[n][2]))
                return f
            pipeline([mk_p1(n) for n in range(len(blocks))], [mk_p2(n) for n in range(len(blocks))])

        def build_hT_tile(m):
            P.op("pool", lambda e: e.tensor_copy(out=hT[:, :, 0:128], in_=hT[:, :, 512:640]),
                 reads=[("hT", 4, k) for k in range(8)], writes=[("hT", 0, k) for k in range(8)])
            blocks = [(m * TB + bi, bi * 128, bi) for bi in range(1, TB)]
            if m < NT - 1:
                blocks.append(((m + 1) * TB, 512, 4))
            build_hT(blocks)
            if m == NT - 1:
                P.op("pool", lambda e: e.memset(hT[:, :, 512:640], 0.0), writes=[("hT", 4, k) for k in range(8)])

        def tile_blocks(m):
            blocks = [(m * TB + bi, bi * 128, bi) for bi in range(1, TB)]
            if m < NT - 1:
                blocks.append(((m + 1) * TB, 512, 4))
            return blocks

        def build_hT_pre(m):
            blocks = tile_blocks(m)
            xi = {}
            hbs = {}
            for n in range(2):
                xi[n] = load_xn(blocks[n][0])
            for n in range(2):
                i = xi[n]
                hbs[n] = norm_p1(lambda i=i: xn[:, i, :], ("xn", i))
            for n in range(2, len(blocks)):
                xi[n] = load_xn(blocks[n][0])
            return blocks, xi, hbs

        def build_hT_pre2(pre):
            blocks, xi, hbs = pre
            for n in range(2, len(blocks)):
                i = xi[n]
                hbs[n] = norm_p1(lambda i=i: xn[:, i, :], ("xn", i))

        def build_hT_post(m, pre):
            blocks, xi, hbs = pre
            P.op("pool", lambda e: e.tensor_copy(out=hT[:, :, 0:128], in_=hT[:, :, 512:640]),
                 reads=[("hT", 4, k) for k in range(8)], writes=[("hT", 0, k) for k in range(8)])
            for n in range(len(blocks)):
                norm_p2(hbs[n], blocks[n][1], 0, hT, ("hT", blocks[n][2]))
            if m == NT - 1:
                P.op("pool", lambda e: e.memset(hT[:, :, 512:640], 0.0), writes=[("hT", 4, k) for k in range(8)])

        def st_A(m):
            last = (m == NT - 1)
            ck("A")
            ws = wload(l, G_KV)
            kvs = []
            for bi in range(1, TB + (0 if last else 1)):
                kvs.append((kv_p1(ws, bi * 128, m * TB + bi), m * TB + bi))

            return kvs
        def st_B(m, kvs, mid=None):
            last = (m == NT - 1)
            ck("B")
            for half in range(2):
                wcc = wload(l, G_CC + half)
                wcx = wload(l, G_CX + half)
                for c4 in range(4):
                    j = half * 4 + c4
                    if j == 6 and mid is not None:
                        mid()
                        ck("B")
                    bcc = fm_chunk(wcc, c4, lambda k: hT[:, k, 1:513], hT_all)
                    bcx = fm_chunk(wcx, c4, lambda k: hT[:, k, 1:513], hT_all)
                    if kvs and j >= 4:
                        kv_p2(*kvs.pop(0))
                    t = next_tmp()
                    P.op("act", lambda e, t=t, bcc=bcc: e.activation(out=tmpf[:, t, :], in_=ps[:, bcc, :], func=AF.Copy),
                         reads=[("ps", bcc)], writes=[("tmp", t)])
                    P.op("dve", lambda e, t=t, bcx=bcx, j=j: e.tensor_tensor(
                        out=u1b[:, j, 2:514], in0=ps[:, bcx, :], in1=tmpf[:, t, :], op=ALU.mult),
                        reads=[("ps", bcx), ("tmp", t)], writes=[("u1b", j)])
        def st_C(m):
            st = {}
            thunks = []
            for half in range(2):
                for c4 in range(4):
                    def th(half=half, c4=c4):
                        ck("C")
                        if c4 == 0:
                            st["wcb"] = wload(l, G_CB + half)
                        wcb = st["wcb"]
                        j = half * 4 + c4
                        t1 = next_tmp()
                        t2 = next_tmp()
                        P.op("dve", lambda e: e.tensor_scalar(
                            out=tmpf[:, t1, :], in0=u1b[:, j, 0:512], scalar1=cw[:, j:j + 1], scalar2=None,
                            op0=ALU.mult),
                            reads=[("u1b", j), "u1h", "cw"], writes=[("tmp", t1)])
                        P.op("dve", lambda e: e.scalar_tensor_tensor(
                            out=tmpf[:, t2, :], in0=u1b[:, j, 1:513], scalar=cw[:, 8 + j:9 + j], in1=tmpf[:, t1, :],
                            op0=ALU.mult, op1=ALU.add),
                            reads=[("u1b", j), "u1h", "cw", ("tmp", t1)], writes=[("tmp", t2)])
                        P.op("dve", lambda e: e.scalar_tensor_tensor(
                            out=tmpf[:, t1, :], in0=u1b[:, j, 2:514], scalar=cw[:, 16 + j:17 + j], in1=tmpf[:, t2, :],
                            op0=ALU.mult, op1=ALU.add),
                            reads=[("u1b", j), "cw", ("tmp", t2)], writes=[("tmp", t1)])
                        bcb = fm_chunk(wcb, c4, lambda k: hT[:, k, 0:512], hT_04)
                        P.op("dve", lambda e: e.tensor_tensor(
                            out=u2[:, j, :], in0=ps[:, bcb, :], in1=tmpf[:, t1, :], op=ALU.mult),
                            reads=[("ps", bcb), ("tmp", t1)], writes=[("u2", j)])
                        if j == 7:
                            P.op("pool", lambda e: e.tensor_copy(out=u1b[:, :, 0:2], in_=u1b[:, :, 512:514]),
                                 reads=[("u1b", jj) for jj in range(8)], writes=["u1h"])
                    thunks.append(th)
            return thunks
        def st_D(m):
            st = {}
            thunks = []
            for half in range(2):
                for c4 in range(4):
                    def t1(half=half, c4=c4):
                        ck("D")
                        if c4 == 0:
                            st["wcp"] = wload(l, G_CP + half)
                            st["wgb"] = wload(l, G_GB + half)
                        st["bg"] = fm_chunk(st["wgb"], c4, lambda k: hT[:, k, 0:512], hT_04)

                    def t2(half=half, c4=c4):
                        ck("D")
                        j = half * 4 + c4
                        bg = st["bg"]
                        by = fm_chunk(st["wcp"], c4, lambda k: u2[:, k, :], [("u2", k) for k in range(8)])
                        t = next_tmp()
                        P.op("act", lambda e: e.activation(out=tmpf[:, t, :], in_=ps[:, bg, :], func=AF.Sigmoid),
                             reads=[("ps", bg)], writes=[("tmp", t)])
                        P.op("dve", lambda e: e.tensor_tensor(
                            out=mixc[:, j, :], in0=ps[:, by, :], in1=tmpf[:, t, :], op=ALU.mult),
                            reads=[("ps", by), ("tmp", t)], writes=[("mixc", j)])
                    thunks += [t1, t2]
            return thunks
        def st_E(m, fillers=()):
            last = (m == NT - 1)
            ck("E")
            wqs = [wload(l, G_Q), wload(l, G_Q + 1)]

            def q_p1(half, bi):
                def f():
                    blk = m * TB + bi
                    wv = Wv(wqs[half])
                    b = next_bank()
                    hres = [("hT", bi, k) for k in range(8)]
                    for k in range(8):
                        P.op("pe", lambda e, k=k: e.matmul(
                            ps[:, b, :], lhsT=hT[:, k, bi * 128:(bi + 1) * 128], rhs=wv[:, k, :],
                            start=(k == 0), stop=(k == 7)),
                            reads=hres + [("W", wqs[half])], writes=[("ps", b)])
                    qq = state["qbf"]
                    state["qbf"] = (qq + 1) % 3
                    rope(b, 8, blk, qbf[:, qq, :], ("qbf", qq), 512)
                    return qq
                return f

            def q_p2(half, bi):
                def f(qq):
                    b2 = next_bank()
                    pbf = ps[:, b2, :].bitcast(BF16)
                    for j in range(4):
                        P.op("pe", lambda e, j=j: e.transpose(
                            out=pbf[:, j * 128:(j + 1) * 128], in_=qbf[:, qq, j * 128:(j + 1) * 128], identity=ident[:]),
                            reads=[("qbf", qq), "ident"], writes=[("ps", b2)])
                    P.op("act", lambda e: e.activation(
                        out=qT[:, half * 4:half * 4 + 4, bi * 128:(bi + 1) * 128],
                        in_=pbf[:, 0:512].rearrange("p (j t) -> p j t", t=128), func=AF.Copy),
                        reads=[("ps", b2)], writes=[("qT", bi, half)])
                return f
            units = [(half, bi) for half in range(2) for bi in range(TB)]
            p1s = [q_p1(*u) for u in units]
            p2s = [q_p2(*u) for u in units]
            outs = [None] * len(units)
            outs[0] = p1s[0]()
            outs[1] = p1s[1]()
            for i in range(len(units)):
                if i + 2 < len(units):
                    ck("E")
                    outs[i + 2] = p1s[i + 2]()
                if i < len(fillers):
                    fillers[i]()
                ck("E")
                p2s[i](outs[i])
        def st_F(m, fillers):
            last = (m == NT - 1)
            ck("F")
            def att_p1(bi, g):
                def f():
                    qb = m * TB + bi
                    kbs = [kb for kb in (qb - 1, qb + 1, qb) if 0 <= kb < NBLK]
                    kc = g // 2
                    pts = []
                    nmask = 0
                    for kb in kbs:
                        b = next_bank("qk")
                        ks = kslot(kb)
                        P.op("pe", lambda e, b=b, ks=ks: e.matmul(
                            ps[:, b, :], lhsT=kTz[:, g % 2, kc, ks:ks + 128],
                            rhs=qT[:, kc * 4:kc * 4 + 4, bi * 128:(bi + 1) * 128], start=True, stop=True),
                            reads=[("kT", kb % 12, g % 2), ("qT", bi, kc)], writes=[("ps", b)])
                        pi = state["pt"]
                        state["pt"] = (pi + 1) % 6
                        P.op("act", lambda e, b=b, pi=pi: e.activation(out=PT[:, pi, :], in_=ps[:, b, :], func=AF.Exp,
                                                                       scale=0.125),
                             reads=[("ps", b)], writes=[("PT", pi)])
                        if kb != qb:
                            mi = 0 if kb < qb else 1
                            ptv = PT[:, pi, :].rearrange("p (h q) -> p h q", q=128)
                            P.op("pool", lambda e, ptv=ptv, mi=mi: e.tensor_tensor(
                                out=ptv, in0=ptv, in1=msk[:, mi, :].unsqueeze(1).broadcast_to([128, 4, 128]),
                                op=ALU.mult),
                                reads=[("PT", pi), "msk"], writes=[("PT", pi)])
                            nmask += 1
                        pts.append((pi, kb))
                    return pts
                return f

            def att_p2(bi, g):
                def f(pts):
                    hr = slice((g % 2) * 64, (g % 2) * 64 + 64)
                    orr = slice(((g + 1) % 2) * 64, ((g + 1) % 2) * 64 + 64)
                    kc = g // 2
                    bo = next_bank("pv")
                    for n_, (pi, kb) in enumerate(pts):
                        vs = vslot(kb)
                        P.op("pe", lambda e, vs=vs, pi=pi, n_=n_: e.matmul(
                            ps[:, bo, :], lhsT=vaug[:, vs, g, :], rhs=PT[:, pi, :], start=(n_ == 0),
                            stop=(n_ == len(pts) - 1)),
                            reads=[("v", vs), ("PT", pi)], writes=[("ps", bo)])
                    if g % 2 == 0:
                        t = next_tmp()
                        pair["t"] = t
                    else:
                        t = pair["t"]
                    dv = tmpf[hr, t, :].rearrange("p (h q) -> p h q", q=128)
                    P.op("dve", lambda e: e.tensor_tensor(
                        out=dv, in0=ps[orr, bo, :].rearrange("p (h q) -> p h q", q=128),
                        in1=snk[orr, g * 4:g * 4 + 4].unsqueeze(2).broadcast_to([64, 4, 128]), op=ALU.add),
                        reads=[("ps", bo), "snk"], writes=[("tmp", t)])

                    def fin(bo=bo, dv=dv, hr=hr, kc=kc, g=g):
                        P.op("dve", lambda e: e.tensor_tensor(
                            out=oT[hr, kc * 4:kc * 4 + 4, bi * 128:(bi + 1) * 128],
                            in0=ps[hr, bo, :].rearrange("p (h q) -> p h q", q=128), in1=dv, op=ALU.mult),
                            reads=[("ps", bo), ("tmp", t)], writes=[("oT", bi, g)])
                    if g % 2 == 0:
                        pair["fin"] = fin
                    else:
                        P.op("dve", lambda e: e.reciprocal(out=tmpf[:, t, :], in_=tmpf[:, t, :]),
                             reads=[("tmp", t)], writes=[("tmp", t)])
                        pair["fin"]()
                        fin()
                return f
            pair = {}
            steps = [(bi, g) for bi in range(TB) for g in range(NKV)]
            p1s = [att_p1(*s_) for s_ in steps]
            p2s = [att_p2(*s_) for s_ in steps]
            cnts = {"qk": 0, "pv": 0, "fm": 0}

            def alloc(kind):
                c = cnts[kind]
                cnts[kind] = c + 1
                if kind == "qk":
                    return c % 3
                if kind == "pv":
                    return (3, 4, 7)[c % 3]
                return 5 + c % 2
            state["alloc"] = alloc
            outs = [None] * len(steps)
            outs[0] = p1s[0]()
            for i in range(len(steps)):
                if i + 1 < len(steps):
                    ck("F")
                    outs[i + 1] = p1s[i + 1]()
                if i < len(fillers):
                    fillers[i]()
                ck("F")
                p2s[i](outs[i])
            state["alloc"] = None
            state["bank"] = 0
        def st_G(m, mid=None):
            last = (m == NT - 1)
            ck("G")
            oT_all = [("oT", bi, g) for bi in range(TB) for g in range(NKV)]
            for half in range(2):
                if half == 1 and mid is not None:
                    mid()
                    ck("G")
                wap = wload(l, G_AP + half)
                wga = wload(l, G_GA + half)
                for c4 in range(4):
                    j = half * 4 + c4
                    bg = fm_chunk(wga, c4, lambda k: hT[:, k, 0:512], hT_04)
                    by = fm_chunk(wap, c4, lambda k: oT[:, k, :], oT_all)
                    t = next_tmp()
                    t2 = next_tmp()
                    P.op("act", lambda e, t=t, bg=bg: e.activation(out=tmpf[:, t, :], in_=ps[:, bg, :], func=AF.Sigmoid),
                         reads=[("ps", bg)], writes=[("tmp", t)])
                    P.op("dve", lambda e, t=t, t2=t2, by=by: e.tensor_tensor(
                        out=tmpf[:, t2, :], in0=ps[:, by, :], in1=tmpf[:, t, :], op=ALU.mult),
                        reads=[("ps", by), ("tmp", t)], writes=[("tmp", t2)])
                    P.op("pool", lambda e, t2=t2, j=j: e.tensor_tensor(
                        out=mixin[:, j, :], in0=tmpf[:, t2, :], in1=mixc[:, j, :], op=ALU.add),
                        reads=[("tmp", t2), ("mixc", j)], writes=[("mixc", j)])
        def st_H(m):
            last = (m == NT - 1)
            ck("H")
            mixin_all = [("mixc", j) for j in range(8)]
            wo = [wload(l, G_OUT), wload(l, G_OUT + 1)]

            def h_p1(bi):
                def f():
                    pr = next_pair()
                    for half in range(2):
                        b = 2 * pr + half
                        wv = Wv(wo[half])
                        for k in range(8):
                            P.op("pe", lambda e, k=k, b=b, wv=wv: e.matmul(
                                ps[:, b, :], lhsT=mixin[:, k, bi * 128:(bi + 1) * 128], rhs=wv[:, k, :],
                                start=(k == 0), stop=(k == 7)),
                                reads=mixin_all + [("W", wo[half])], writes=[("ps", b)])
                    post_norm_residual(bi, 0, pr)
                    return norm_p1(lambda: xt[:, bi, :], ("xt", bi))
                return f

            def h_p2(bi):
                def f(hb):
                    norm_p2(hb, bi * 128, 1, h2T, ("h2T", bi))
                return f
            hbs = [h_p1(bi)() for bi in range(TB)]
            return [(lambda bi=bi: h_p2(bi)(hbs[bi])) for bi in range(TB)]
        def st_FFN(m, l):
            last = (m == NT - 1)
            ck("FFN")
            if l + 1 < L:
                for g in PREP_ORDER[5 * m:5 * m + 5]:
                    prep(l + 1, g)
            h2_all = [("h2T", bi, k) for bi in range(TB) for k in range(8)]
            for gi in range(6):
                wg = wload(l, G_G + gi)
                wu = wload(l, G_U + gi)
                for c4 in range(4 if gi < 5 else 2):
                    c = gi * 4 + c4
                    bgate = fm_chunk(wg, c4, lambda k: h2T[:, k, :], h2_all)
                    bup = fm_chunk(wu, c4, lambda k: h2T[:, k, :], h2_all)
                    t = next_tmp()
                    P.op("act", lambda e, t=t, bgate=bgate: e.activation(out=tmpf[:, t, :], in_=ps[:, bgate, :],
                                                                         func=AF.Silu),
                         reads=[("ps", bgate)], writes=[("tmp", t)])
                    P.op("dve", lambda e, t=t, bup=bup, c=c: e.tensor_tensor(
                        out=aT[:, c, :], in0=ps[:, bup, :], in1=tmpf[:, t, :], op=ALU.mult),
                        reads=[("ps", bup), ("tmp", t)], writes=[("aT", c)])
            aT_all = [("aT", c) for c in range(NFC)]
            for half in range(2):
                for r in range(3):
                    wd = wload(l, G_D + half * 3 + r)
                    wv = Wv(wd)
                    nkc = 8 if r < 2 else 6
                    for bi in range(TB):
                        b = 2 * bi + half
                        for kl in range(nkc):
                            kc = r * 8 + kl
                            P.op("pe", lambda e, b=b, bi=bi, kl=kl, kc=kc, wv=wv: e.matmul(
                                ps[:, b, :], lhsT=aT[:, kc, bi * 128:(bi + 1) * 128], rhs=wv[:, kl, :],
                                start=(kc == 0), stop=(kc == NFC - 1)),
                                reads=[("aT", kc), ("W", wd)], writes=[("ps", b)])
            state["bank"] = 0
            for bi in range(TB):
                post_norm_residual(bi, 1, bi)
            P.op("pool", lambda e, m=m: e.dma_start(
                out=x_dst[m * 512:(m + 1) * 512, :].rearrange("(b p) d -> p b d", p=128), in_=xt[:]),
                reads=[("xt", bi) for bi in range(TB)],
                writes=[("x", id(x_dst), m * TB + bi) for bi in range(TB)], dma="xst")
            if not last:
                P.op("pool", lambda e, m=m: e.dma_start(
                    out=xt[:], in_=x_src[(m + 1) * 512:(m + 2) * 512, :].rearrange("(b p) d -> p b d", p=128)),
                    reads=[("x", id(x_src), (m + 1) * TB + bi) for bi in range(TB)],
                    writes=[("xt", bi) for bi in range(TB)], dma="xt")


        def load_params():
            P.op("pool", lambda e: e.dma_start(out=gcol[:], in_=gcol_d[l]), writes=["gcol"], dma="p0")
            P.op("pool", lambda e: e.dma_start(out=cw[:], in_=cw_d[l]), writes=["cw"], dma="p2")
            P.op("pool", lambda e: e.dma_start(out=snk[:], in_=sink_d[l]), writes=["snk0"], dma="p3")
            P.op("act", lambda e: e.activation(out=snk[:], in_=snk[:], func=AF.Exp), reads=["snk0"], writes=["snk0", "snk"])

        def prologue():
            ck("start")
            if l > 0:
                load_params()

            build_hT([(0, 512, 4)])
            pre0 = build_hT_pre(0)
            ws = wload(l, G_KV)
            kk0 = kv_p1(ws, 512, 0)
            kv_p2(kk0, 0)
            build_hT_pre2(pre0)
            P.op("pool", lambda e: e.memset(u1b[:, :, 0:2], 0.0), writes=["u1h"])
            for half in range(2):
                wcc = wload(l, G_CC + half)
                wcx = wload(l, G_CX + half)
                bcc = next_bank()
                bcx = next_bank()
                for (bb, wsl) in ((bcc, wcc), (bcx, wcx)):
                    wv = Wv(wsl)
                    for c4 in range(4):
                        for k in range(8):
                            P.op("pe", lambda e, k=k, c4=c4, wv=wv, bb=bb: e.matmul(
                                ps[:, bb, c4:c4 + 1], lhsT=wv[:, k, c4 * 128:(c4 + 1) * 128], rhs=hT[:, k, 512:513],
                                start=(k == 0), stop=(k == 7)),
                                reads=[("W", wsl)] + [("hT", 4, k2) for k2 in range(8)], writes=[("ps", bb)])
                t = next_tmp()
                P.op("act", lambda e, t=t, bcc=bcc: e.activation(out=tmpf[:, t, 0:4], in_=ps[:, bcc, 0:4], func=AF.Copy),
                     reads=[("ps", bcc)], writes=[("tmp", t)])
                P.op("dve", lambda e, t=t, bcx=bcx, half=half: e.tensor_tensor(
                    out=u1b[:, half * 4:half * 4 + 4, 1], in0=ps[:, bcx, 0:4], in1=tmpf[:, t, 0:4], op=ALU.mult),
                    reads=[("ps", bcx), ("tmp", t)], writes=["u1h"])
            build_hT_post(0, pre0)

            st_B(0, st_A(0))

        def body(next_prologue):
            P.op("pool", lambda e: e.dma_start(out=gpo[:], in_=gpost_d[l]), writes=["gpo"], dma="p1")
            P.op("pool", lambda e: e.dma_start(
                out=xt[:], in_=x_src[0:512, :].rearrange("(b p) d -> p b d", p=128)),
                reads=[("x", id(x_src), bi) for bi in range(TB)],
                writes=[("xt", bi) for bi in range(TB)], dma="xt")

            for m in range(NT):
                last = (m == NT - 1)
                st_E(m, st_C(m))
                st_F(m, st_D(m))
                pre = None if last else build_hT_pre(m + 1)
                st_G(m, None if last else (lambda: build_hT_pre2(pre)))
                if not last:
                    build_hT_post(m + 1, pre)
                h2 = st_H(m)

                def run_h2():
                    ck("H2")
                    for f in h2:
                        f()
                if not last:
                    st_B(m + 1, st_A(m + 1), run_h2)
                else:
                    run_h2()
                if last and next_prologue is not None:
                    next_prologue()
                st_FFN(m, l)


        return prologue, body, load_params

    def post_norm_residual(bi, which, pr):
        st = next_stat()
        src = ps[:, 2 * pr:2 * pr + 2, :].rearrange("p a c -> p (a c)")
        rr = [("ps", 2 * pr), ("ps", 2 * pr + 1)]
        tp = state["nrm"]
        state["nrm"] = 1 - tp
        nrm = tmpf[:, 2 * tp:2 * tp + 2, :].rearrange("p a c -> p (a c)")
        nres = [("tmp", 2 * tp), ("tmp", 2 * tp + 1)]
        P.op("act", lambda e: e.activation(out=nrm, in_=src, func=AF.Square, scale=1.0 / 32.0,
                                           accum_out=stat[:, st:st + 1]),
             reads=rr, writes=nres + [("stat", st)])
        P.op("act", lambda e: e.activation(out=stat[:, st:st + 1], in_=stat[:, st:st + 1], func=AF.Sqrt,
                                           bias=epsb[:, 0:1], scale=1.0),
             reads=[("stat", st), "epsb"], writes=[("stat", st)])
        P.op("dve", lambda e: e.reciprocal(out=stat[:, st:st + 1], in_=stat[:, st:st + 1]),
             reads=[("stat", st)], writes=[("stat", st)])
        P.op("dve", lambda e: e.scalar_tensor_tensor(
            out=nrm, in0=src, scalar=stat[:, st:st + 1], in1=gpo[:, which * 1024:(which + 1) * 1024],
            op0=ALU.mult, op1=ALU.mult),
            reads=rr + [("stat", st), "gpo"], writes=nres)
        P.op("pool", lambda e: e.tensor_tensor(out=xt[:, bi, :], in0=xt[:, bi, :], in1=nrm, op=ALU.add),
             reads=nres + [("xt", bi)], writes=[("xt", bi)])

    try:
        lays = [make_layer(l, x_in if l == 0 else xs, out_d if l == L - 1 else xs) for l in range(L)]
        lays[0][2]()
        for g in PREP_ORDER[:5]:
            prep(0, g)
        lays[0][0]()
        for g in PREP_ORDER[5:]:
            prep(0, g)
        for l in range(L):
            lays[l][1](lays[l + 1][0] if l + 1 < L else None)
    except _Stop:
        pass
    final = [o for o in P.ops if o.dma == "xst"][-1:]
    if not final:
        final = [o for o in P.ops if o.dma is not None][-1:]
    P.emit(final)
    es.close()
    nc._prog_ops = P.ops
    nc._sem_keys = P.sem_keys
    return nc


def _head_perm():
    heads = []
    for j in range(8):
        if j < 4:
            heads += [j, 4 + j]
        else:
            heads += [8 + (j - 4), 12 + (j - 4)]
    idx = np.concatenate([np.arange(h * 64, (h + 1) * 64) for h in heads])
    return idx


def _grp(mat, c0, ncols=512):
    k = mat.shape[0] // 128
    g = np.zeros((128, 8, 512), np.float32)
    blk = mat[:, c0:c0 + ncols].reshape(k, 128, -1).transpose(1, 0, 2)
    g[:, :k, :blk.shape[2]] = blk
    return g.reshape(128, 4096)


def pack_weights(w_in, w_attn_proj, w_conv_proj, w_out, w_gate, w_up, w_down, n_layers):
    perm = _head_perm()
    out = np.zeros((n_layers, NG, 128, 4096), np.float32)
    for l in range(n_layers):
        wi = w_in[l]
        q = wi[:, 0:1024][:, perm]
        out[l, G_KV] = _grp(wi, 1024)
        for h in range(2):
            out[l, G_Q + h] = _grp(q, h * 512)
            out[l, G_CB + h] = _grp(wi, 1536 + h * 512)
            out[l, G_CC + h] = _grp(wi, 2560 + h * 512)
            out[l, G_CX + h] = _grp(wi, 3584 + h * 512)
            out[l, G_GA + h] = _grp(wi, 4608 + h * 512)
            out[l, G_GB + h] = _grp(wi, 5632 + h * 512)
            out[l, G_AP + h] = _grp(w_attn_proj[l][perm, :], h * 512)
            out[l, G_CP + h] = _grp(w_conv_proj[l], h * 512)
            out[l, G_OUT + h] = _grp(w_out[l], h * 512)
        for gi in range(6):
            out[l, G_G + gi] = _grp(w_gate[l], gi * 512)
            out[l, G_U + gi] = _grp(w_up[l], gi * 512)
        for h in range(2):
            for r in range(3):
                rows = w_down[l][r * 1024:min((r + 1) * 1024, DFF), :]
                out[l, G_D + h * 3 + r] = _grp(rows, h * 512)
    return out


def host_consts():
    pos = np.arange(S, dtype=np.float32)
    inv = np.power(np.float32(500000.0), -np.arange(0, 16, 2, dtype=np.float32) / np.float32(16)).astype(np.float32)
    ang = (pos[:, None] * inv[None, :]).astype(np.float32)
    cos = np.cos(ang).astype(np.float32).reshape(NBLK, 128, 8).transpose(1, 0, 2).reshape(128, NBLK * 8)
    sin = np.sin(ang).astype(np.float32).reshape(NBLK, 128, 8).transpose(1, 0, 2).reshape(128, NBLK * 8)
    kk = np.arange(128)[:, None]
    qq = np.arange(128)[None, :]
    cst = np.concatenate([np.eye(128, dtype=np.float32), (kk >= qq).astype(np.float32),
                          (kk <= qq).astype(np.float32)], axis=1)
    return np.ascontiguousarray(cos), np.ascontiguousarray(sin), np.ascontiguousarray(cst)


def small_params(g_pre_mix, g_post_mix, g_pre_ffn, g_post_ffn, conv_w, attn_sink, n_layers):
    L = n_layers
    gcol = np.zeros((L, 128, 16), np.float32)
    gpost = np.zeros((L, 128, 2048), np.float32)
    cw = np.zeros((L, 128, 24), np.float32)
    sink = np.zeros((L, 128, 16), np.float32)
    for l in range(L):
        gcol[l, :, 0:8] = g_pre_mix[l].reshape(8, 128).T
        gcol[l, :, 8:16] = g_pre_ffn[l].reshape(8, 128).T
        gpost[l, :, 0:1024] = g_post_mix[l][None, :]
        gpost[l, :, 1024:2048] = g_post_ffn[l][None, :]
        for t in range(3):
            cw[l, :, t * 8:(t + 1) * 8] = conv_w[l, t].reshape(8, 128).T
        sink[l] = attn_sink[l][None, :]
    return gcol, gpost, cw, sink


_NC_CACHE = {}


def run_layers(x, params, n_layers, n_cores=8, trace=False):
    (g_pre_mix, w_in, attn_sink, conv_w, w_attn_proj, w_conv_proj, w_out, g_post_mix, g_pre_ffn,
     w_gate, w_up, w_down, g_post_ffn) = [np.asarray(p, np.float32) for p in params]
    if n_layers not in _NC_CACHE:
        _NC_CACHE[n_layers] = build_program(n_layers)
    nc = _NC_CACHE[n_layers]
    wpack = pack_weights(w_in, w_attn_proj, w_conv_proj, w_out, w_gate, w_up, w_down, n_layers)
    cos, sin, cst = host_consts()
    gcol, gpost, cw, sink = small_params(g_pre_mix, g_post_mix, g_pre_ffn, g_post_ffn, conv_w, attn_sink, n_layers)
    in_maps = []
    for c in range(n_cores):
        in_maps.append({"x": np.ascontiguousarray(x[c]), "wpack": wpack, "gcol": gcol, "gpost": gpost, "cw": cw,
                        "sink": sink, "cos": cos, "sin": sin, "cst": cst})
    res = run_bass_kernel_spmd(nc, in_maps, core_ids=list(range(n_cores)), trace=trace)
    out = np.stack([np.asarray(r["out"]) for r in res.results], axis=0)
    return out, res


def kernel(x, g_pre_mix, w_in, attn_sink, conv_w, w_attn_proj, w_conv_proj, w_out,
           g_post_mix, g_pre_ffn, w_gate, w_up, w_down, g_post_ffn):
    x = np.asarray(x, np.float32)
    params = (g_pre_mix, w_in, attn_sink, conv_w, w_attn_proj, w_conv_proj, w_out, g_post_mix, g_pre_ffn,
              w_gate, w_up, w_down, g_post_ffn)
    out, _ = run_layers(x, params, DEPTH, n_cores=8)
    return out.astype(np.float32)
```
